# Optimizing a Trainium2 kernel written in Bass

```python
import math
import jax, jax.numpy as jnp
from jax import lax
import numpy as np

D_MODEL = 1024
BATCH = 4
SEQ = 8192
DEPTH = 1
DEC_BATCH = 32
DEC_SEQ = 16
PAST_LEN = 1024

CHUNK = 64
N_BAND_CHUNKS = 8
HEAD_DIM = 64
A_HEADS = 4
A_WIDTH = A_HEADS * HEAD_DIM
REL_CLIP = 128
B_HEADS = 4
B_VDIM = 2 * HEAD_DIM
B_QK_WIDTH = B_HEADS * 2 * HEAD_DIM
B_WIDTH = B_HEADS * B_VDIM
M_HEADS = 4
M_WIDTH = M_HEADS * HEAD_DIM
N_MEM = 256
MIX_WIDTH = A_WIDTH + B_WIDTH + M_WIDTH
PROJ_WIDTHS = (A_WIDTH, A_WIDTH, A_WIDTH, B_QK_WIDTH, B_QK_WIDTH, B_WIDTH, M_WIDTH, MIX_WIDTH)
PROJ_TOTAL = sum(PROJ_WIDTHS)
PROJ_SPLITS = tuple(int(s) for s in np.cumsum(PROJ_WIDTHS)[:-1])
ROPE_THETA = 10000.0
Q_BLOCK = 128
DEEPNORM_ALPHA = (2.0 * DEPTH) ** 0.25
DEEPNORM_BETA = (8.0 * DEPTH) ** -0.25
LN_EPS = 1e-5
RMS_EPS = 1e-5
NEG_INF = -1e30

kernel_name = 'hybrid_stream_band_diff_mem_step'


def layer_norm(x, g, b):
    xf = x.astype(jnp.float32)
    mu = jnp.mean(xf, -1, keepdims=True)
    var = jnp.mean(jnp.square(xf - mu), -1, keepdims=True)
    return ((xf - mu) * lax.rsqrt(var + LN_EPS) * g.astype(jnp.float32) + b.astype(jnp.float32)).astype(x.dtype)


def rope(x, pos):
    half = x.shape[-1] // 2
    inv = ROPE_THETA ** (-jnp.arange(half, dtype=jnp.float32) / half)
    ang = pos.astype(jnp.float32)[:, None] * inv[None, :]
    shape = (pos.shape[0],) + (1,) * (x.ndim - 3) + (half,)
    cos = jnp.cos(ang).reshape(shape)
    sin = jnp.sin(ang).reshape(shape)
    x1 = x[..., :half].astype(jnp.float32)
    x2 = x[..., half:].astype(jnp.float32)
    return jnp.concatenate([x1 * cos - x2 * sin, x2 * cos + x1 * sin], -1).astype(x.dtype)


def split_projection(x, w_in, pos):
    Bt, S, _ = x.shape
    h = jnp.einsum('bsd,de->bse', x, w_in)
    a_q, a_k, a_v, b_q, b_k, b_v, m_q, gate = jnp.split(h, PROJ_SPLITS, axis=-1)
    a_q = a_q.reshape(Bt, S, A_HEADS, HEAD_DIM)
    a_k = a_k.reshape(Bt, S, A_HEADS, HEAD_DIM)
    a_v = a_v.reshape(Bt, S, A_HEADS, HEAD_DIM)
    b_q = rope(b_q.reshape(Bt, S, B_HEADS, 2, HEAD_DIM), pos)
    b_k = rope(b_k.reshape(Bt, S, B_HEADS, 2, HEAD_DIM), pos)
    b_v = b_v.reshape(Bt, S, B_HEADS, B_VDIM)
    m_q = m_q.reshape(Bt, S, M_HEADS, HEAD_DIM)
    return a_q, a_k, a_v, b_q, b_k, b_v, m_q, gate


def rel_position_bias(table, dist):
    return table[:, jnp.clip(dist, -REL_CLIP, REL_CLIP) + REL_CLIP].astype(jnp.float32)


def band_attention_prompt(q, k, v, rel_bias):
    Bt, S, H, d = q.shape
    nc = S // CHUNK
    nb = N_BAND_CHUNKS + 1
    pad = ((0, 0), (N_BAND_CHUNKS, 0), (0, 0), (0, 0), (0, 0))
    kc = jnp.pad(k.reshape(Bt, nc, CHUNK, H, d), pad)
    vc = jnp.pad(v.reshape(Bt, nc, CHUNK, H, d), pad)
    k_band = jnp.concatenate([kc[:, j:j + nc] for j in range(nb)], axis=2)
    v_band = jnp.concatenate([vc[:, j:j + nc] for j in range(nb)], axis=2)
    qc = q.reshape(Bt, nc, CHUNK, H, d)
    s = jnp.einsum('bcqhd,bckhd->bchqk', qc, k_band).astype(jnp.float32) * (d ** -0.5)
    qi = jnp.arange(CHUNK)
    ki = jnp.arange(nb * CHUNK)
    dist = qi[:, None] + N_BAND_CHUNKS * CHUNK - ki[None, :]
    s = s + rel_position_bias(rel_bias, dist)[None, None]
    valid = (jnp.arange(nc)[:, None] + ki[None, :] // CHUNK) >= N_BAND_CHUNKS
    s = jnp.where(valid[None, :, None, None, :], s, NEG_INF)
    p = jax.nn.softmax(s, axis=-1)
    o = jnp.einsum('bchqk,bckhd->bcqhd', p.astype(v.dtype), v_band)
    return o.reshape(Bt, S, H, d)


def band_attention_sample(q, k_new, v_new, k_cache, v_cache, rel_bias):
    T = q.shape[1]
    R = k_cache.shape[1]
    k = jnp.concatenate([k_cache, k_new], axis=1)
    v = jnp.concatenate([v_cache, v_new], axis=1)
    s = jnp.einsum('bqhd,bkhd->bhqk', q, k).astype(jnp.float32) * (q.shape[-1] ** -0.5)
    dist = jnp.arange(T)[:, None] + R - jnp.arange(R + T)[None, :]
    s = s + rel_position_bias(rel_bias, dist)[None]
    p = jax.nn.softmax(s, axis=-1)
    return jnp.einsum('bhqk,bkhd->bqhd', p.astype(v.dtype), v)


def diff_lambda_value(lp, lambda_init):
    lpf = lp.astype(jnp.float32)
    return jnp.exp(jnp.sum(lpf[0] * lpf[1])) - jnp.exp(jnp.sum(lpf[2] * lpf[3])) + lambda_init


def diff_core(q, k, v, lam, mask):
    s = jnp.einsum('bqhmd,bkhmd->bhmqk', q, k).astype(jnp.float32) * (q.shape[-1] ** -0.5)
    if mask is not None:
        s = jnp.where(mask, s, NEG_INF)
    p = jax.nn.softmax(s, axis=-1)
    w = p[:, :, 0] - lam * p[:, :, 1]
    return jnp.einsum('bhqk,bkhe->bqhe', w.astype(v.dtype), v)


def diff_attention_prompt(q, k, v, lam):
    Bt, S, H, _, d = q.shape
    nblk = S // Q_BLOCK
    qb = jnp.moveaxis(q.reshape(Bt, nblk, Q_BLOCK, H, 2, d), 1, 0)
    k_chunk = jnp.arange(S) // CHUNK

    def block(args):
        qi, i = args
        q_chunk = (i * Q_BLOCK + jnp.arange(Q_BLOCK)) // CHUNK
        mask = k_chunk[None, :] <= q_chunk[:, None]
        return diff_core(qi, k, v, lam, mask)

    out = lax.map(block, (qb, jnp.arange(nblk)))
    return jnp.moveaxis(out, 0, 1).reshape(Bt, S, H, v.shape[-1])


def diff_post(o, g, lambda_init):
    of = o.astype(jnp.float32)
    of = of * lax.rsqrt(jnp.mean(of * of, -1, keepdims=True) + RMS_EPS) * g.astype(jnp.float32) * (1.0 - lambda_init)
    return of.astype(o.dtype)


def memory_kv(mem, w_mem_kv):
    Bt, N, _ = mem.shape
    kv = jnp.einsum('bnd,de->bne', mem, w_mem_kv)
    mk, mv = jnp.split(kv, 2, axis=-1)
    return mk.reshape(Bt, N, M_HEADS, HEAD_DIM), mv.reshape(Bt, N, M_HEADS, HEAD_DIM)


def memory_attention(q, mk, mv):
    s = jnp.einsum('bqhd,bkhd->bhqk', q, mk).astype(jnp.float32) * (q.shape[-1] ** -0.5)
    p = jax.nn.softmax(s, axis=-1)
    return jnp.einsum('bhqk,bkhd->bqhd', p.astype(mv.dtype), mv)


def merge_output(x, o_a, o_b, o_m, gate, w_out, g, b):
    Bt, S, _ = x.shape
    mixed = jnp.concatenate([o_a.reshape(Bt, S, A_WIDTH), o_b.reshape(Bt, S, B_WIDTH), o_m.reshape(Bt, S, M_WIDTH)], -1)
    y = jnp.einsum('bse,ed->bsd', mixed * jax.nn.silu(gate), w_out)
    return layer_norm(DEEPNORM_ALPHA * x + y, g, b)


def setup_inputs(seed: int = 0) -> dict:
    key = jax.random.key(seed)
    ks = jax.random.split(key, 18)
    a_rows = min(N_BAND_CHUNKS * CHUNK, PAST_LEN)
    nrm = jax.random.normal
    return {
        'x_prompt': nrm(ks[0], (BATCH, SEQ, D_MODEL), jnp.float32),
        'x_sample': nrm(ks[1], (DEC_BATCH, DEC_SEQ, D_MODEL), jnp.float32),
        'cache_a_k': nrm(ks[2], (DEPTH, DEC_BATCH, a_rows, A_HEADS, HEAD_DIM), jnp.float32),
        'cache_a_v': nrm(ks[3], (DEPTH, DEC_BATCH, a_rows, A_HEADS, HEAD_DIM), jnp.float32),
        'cache_b_k': nrm(ks[4], (DEPTH, DEC_BATCH, PAST_LEN, B_HEADS, 2, HEAD_DIM), jnp.float32),
        'cache_b_v': nrm(ks[5], (DEPTH, DEC_BATCH, PAST_LEN, B_HEADS, B_VDIM), jnp.float32),
        'cache_mem_k': nrm(ks[6], (DEPTH, DEC_BATCH, N_MEM, M_HEADS, HEAD_DIM), jnp.float32),
        'cache_mem_v': nrm(ks[7], (DEPTH, DEC_BATCH, N_MEM, M_HEADS, HEAD_DIM), jnp.float32),
        'mem_prompt': nrm(ks[8], (BATCH, N_MEM, D_MODEL), jnp.float32),
        'w_in': nrm(ks[9], (DEPTH, D_MODEL, PROJ_TOTAL), jnp.float32) * D_MODEL ** -0.5,
        'w_mem_kv': nrm(ks[10], (DEPTH, D_MODEL, 2 * M_WIDTH), jnp.float32) * D_MODEL ** -0.5,
        'a_rel_bias': nrm(ks[11], (DEPTH, A_HEADS, 2 * REL_CLIP + 1), jnp.float32) * 0.5,
        'diff_lambda': nrm(ks[12], (DEPTH, 4, HEAD_DIM), jnp.float32) * 0.1,
        'diff_subln_g': 1.0 + 0.05 * nrm(ks[13], (DEPTH, B_VDIM), jnp.float32),
        'w_out': nrm(ks[14], (DEPTH, MIX_WIDTH, D_MODEL), jnp.float32) * (MIX_WIDTH ** -0.5) * DEEPNORM_BETA,
        'ln_g': 1.0 + 0.05 * nrm(ks[15], (DEPTH, D_MODEL), jnp.float32),
        'ln_b': 0.05 * nrm(ks[16], (DEPTH, D_MODEL), jnp.float32),
    }


def reference(x_prompt, x_sample, cache_a_k, cache_a_v, cache_b_k, cache_b_v, cache_mem_k, cache_mem_v, mem_prompt,
              w_in, w_mem_kv, a_rel_bias, diff_lambda, diff_subln_g, w_out, ln_g, ln_b):
    S = x_prompt.shape[1]
    T = x_sample.shape[1]
    past = cache_b_k.shape[2]
    pos_p = jnp.arange(S)
    pos_s = past + jnp.arange(T)
    a_rows_p = min(N_BAND_CHUNKS * CHUNK, S)
    xp, xs = x_prompt, x_sample
    ak_p, av_p, bk_p, bv_p, mk_p, mv_p = [], [], [], [], [], []
    ak_s, av_s, bk_s, bv_s = [], [], [], []
    for layer in range(DEPTH):
        lambda_init = 0.8 - 0.6 * math.exp(-0.3 * layer)
        lam = diff_lambda_value(diff_lambda[layer], lambda_init)
        aq, ak, av, bq, bk, bv, mq, gate = split_projection(xp, w_in[layer], pos_p)
        mk, mv = memory_kv(mem_prompt, w_mem_kv[layer])
        o_a = band_attention_prompt(aq, ak, av, a_rel_bias[layer])
        o_b = diff_post(diff_attention_prompt(bq, bk, bv, lam), diff_subln_g[layer], lambda_init)
        o_m = memory_attention(mq, mk, mv)
        xp_next = merge_output(xp, o_a, o_b, o_m, gate, w_out[layer], ln_g[layer], ln_b[layer])
        ak_p.append(ak[:, S - a_rows_p:])
        av_p.append(av[:, S - a_rows_p:])
        bk_p.append(bk)
        bv_p.append(bv)
        mk_p.append(mk)
        mv_p.append(mv)
        sq, sk, sv, tq, tk, tv, nq, sgate = split_projection(xs, w_in[layer], pos_s)
        s_a = band_attention_sample(sq, sk, sv, cache_a_k[layer], cache_a_v[layer], a_rel_bias[layer])
        tk_all = jnp.concatenate([cache_b_k[layer], tk], axis=1)
        tv_all = jnp.concatenate([cache_b_v[layer], tv], axis=1)
        s_b = diff_post(diff_core(tq, tk_all, tv_all, lam, None), diff_subln_g[layer], lambda_init)
        s_m = memory_attention(nq, cache_mem_k[layer], cache_mem_v[layer])
        xs = merge_output(xs, s_a, s_b, s_m, sgate, w_out[layer], ln_g[layer], ln_b[layer])
        ak_s.append(sk)
        av_s.append(sv)
        bk_s.append(tk)
        bv_s.append(tv)
        xp = xp_next
    return (xp, xs,
            jnp.stack(ak_p), jnp.stack(av_p), jnp.stack(bk_p), jnp.stack(bv_p), jnp.stack(mk_p), jnp.stack(mv_p),
            jnp.stack(ak_s), jnp.stack(av_s), jnp.stack(bk_s), jnp.stack(bv_s))
```

```python
import numpy as np
from contextlib import ExitStack
import concourse.bass as bass
import concourse.mybir as mybir
from concourse.bass_utils import run_bass_kernel_spmd

F32 = mybir.dt.float32
BF16 = mybir.dt.bfloat16
AF = mybir.ActivationFunctionType
ALU = mybir.AluOpType
AX = mybir.AxisListType
NEG = -30000.0
ALPHA = 2.0 ** 0.25
LAMBDA_INIT = 0.8 - 0.6
N_CORES = 8


class Op:
    __slots__ = ("eng", "fn", "deps", "signal", "is_dma", "dsem", "dval", "prewait", "cnt")

    def __init__(self, eng, fn):
        self.eng = eng; self.fn = fn; self.deps = []; self.signal = False
        self.is_dma = False; self.dsem = None; self.dval = 0; self.prewait = None; self.cnt = 0


class Prog:
    ND = 8

    def __init__(self, nc, es):
        self.nc = nc
        self.engs = ("pe", "act", "dve", "pool", "sp")
        self.ops = {e: [] for e in self.engs}
        self.res = {}
        self.sems = {e: es.enter_context(nc.semaphore("s_" + e)) for e in self.engs}
        self.dsems = {q: [es.enter_context(nc.semaphore(f"d_{q}{i}")) for i in range(self.ND)] for q in ("sp", "pool")}
        self.dcount = {"sp": 0, "pool": 0}
        self.dmas = {"sp": [], "pool": []}
        self.bar = []

    def op(self, eng, fn, r=(), w=()):
        o = Op(eng, fn)
        deps = set(self.bar)
        for k in r:
            e = self.res.get(k)
            if e is not None and e[0] is not None:
                deps.add(e[0])
        for k in w:
            e = self.res.setdefault(k, [None, []])
            if e[0] is not None:
                deps.add(e[0])
            deps.update(e[1])
        for k in r:
            e = self.res.setdefault(k, [None, []])
            e[1] = [x for x in e[1] if x.is_dma or x.eng != eng]
            e[1].append(o)
        for k in w:
            e = self.res[k]
            e[0] = o; e[1] = []
        o.deps = [d for d in deps if d.is_dma or not (d.eng == "pe" and eng == "pe")]
        for d in o.deps:
            if not d.is_dma:
                d.signal = True
        self.ops[eng].append(o)
        return o

    def dma(self, q, out, in_, r=(), w=()):
        o = self.op(q, lambda e: e.dma_start(out=out, in_=in_), r=r, w=w)
        o.is_dma = True
        k = self.dcount[q]; self.dcount[q] += 1
        o.dsem = self.dsems[q][k % self.ND]
        o.dval = 16 * (k // self.ND + 1)
        if k >= self.ND:
            o.prewait = (o.dsem, 16 * (k // self.ND))
        self.dmas[q].append(o)
        return o

    def barrier(self):
        b = []
        for e in self.engs:
            for o in reversed(self.ops[e]):
                if not o.is_dma:
                    o.signal = True
                    b.append(o)
                    break
        for q in ("sp", "pool"):
            b.extend(self.dmas[q][-self.ND:])
        self.bar = b
        self.res = {}

    def emit(self, block):
        for e, lst in self.ops.items():
            c = 0
            for o in lst:
                if o.signal and not o.is_dma:
                    c += 1
                o.cnt = c

        def run(eng_name):
            def body(e):
                waited = {}

                def w(sem, val):
                    key = id(sem)
                    if waited.get(key, 0) < val:
                        e.wait_ge(sem, val); waited[key] = val
                for o in self.ops[eng_name]:
                    for d in o.deps:
                        if d.is_dma:
                            w(d.dsem, d.dval)
                        else:
                            w(self.sems[d.eng], d.cnt)
                    if o.is_dma and o.prewait is not None:
                        w(*o.prewait)
                    ins = o.fn(e)
                    if o.is_dma:
                        ins.then_inc(o.dsem, 16)
                    elif o.signal:
                        ins.then_inc(self.sems[o.eng], 1)
                if eng_name == "sp":
                    for q in ("sp", "pool"):
                        n = self.dcount[q]
                        for j in range(self.ND):
                            cntj = (n - j + self.ND - 1) // self.ND
                            if cntj > 0:
                                w(self.dsems[q][j], 16 * cntj)
            return body
        block.tensor(run("pe"))
        block.scalar(run("act"))
        block.vector(run("dve"))
        block.gpsimd(run("pool"))
        block.sync(run("sp"))


class Arena:
    def __init__(self, ap, size):
        self.ap = ap; self.size = size; self.off = 0

    def reset(self):
        self.off = 0

    def alloc(self, n):
        n = (n + 15) // 16 * 16
        assert self.off + n <= self.size, ("arena overflow", self.off, n, self.size)
        a = self.ap[:, self.off:self.off + n]
        self.off += n
        return a


def build(nspan):
    S = 1024 * nspan
    NT = S // 128
    NOWN = S // 2
    NU = NOWN // 128
    nc = bass.Bass("TRN2", target_bir_lowering=False)

    def din(name, shape):
        return nc.dram_tensor(name, list(shape), F32, kind="ExternalInput").ap()

    def dout(name, shape):
        return nc.dram_tensor(name, list(shape), F32, kind="ExternalOutput").ap()

    def dscr(name, shape):
        return nc.dram_tensor(name, list(shape), BF16, kind="Internal").ap()

    xT_all = din("xT_all", [1024, S]); xT_own = din("xT_own", [1024, NOWN]); x_own = din("x_own", [NOWN, 1024])
    memT = din("memT", [1024, 256])
    w_in = din("w_in", [1024, 3584]); w_mem = din("w_mem", [1024, 512]); w_out = din("w_out", [1024, 1024])
    cs_kv = din("cs_kv", [128, NT * 64]); cs_q = din("cs_q", [128, NU * 64]); cs_s = din("cs_s", [16, 64])
    abias = din("abias", [128, 4 * 5 * 64]); sabias = din("sabias", [128, 4 * 5 * 16])
    dbias = din("dbias", [128, 1])
    dlam = din("dlam", [1, 256]); gsub = din("gsub", [1, 128]); lng = din("lng", [1, 1024]); lnb = din("lnb", [1, 1024])
    xsT = din("xsT", [1024, 64]); xs = din("xs", [64, 1024])
    cbkT = din("cbkT", [4, 4, 128, 1024]); cbv = din("cbv", [4, 128, 4096])
    cakT = din("cakT", [4, 2, 128, 512]); cav = din("cav", [4, 128, 1024])
    cmkT = din("cmkT", [4, 2, 128, 256]); cmv = din("cmv", [4, 128, 512])

    y_own = dout("y_own", [NOWN, 1024]); ys = dout("ys", [64, 1024])
    bk_own = dout("bk_own", [NOWN, 512]); bv_own = dout("bv_own", [NOWN, 512])
    ak_own = dout("ak_own", [256, 256]); av_own = dout("av_own", [256, 256])
    mk_o = dout("mk_o", [256, 256]); mv_o = dout("mv_o", [256, 256])
    sak = dout("sak", [64, 256]); sav = dout("sav", [64, 256]); sbk = dout("sbk", [64, 512]); sbv = dout("sbv", [64, 512])

    bkT_s = dscr("bkT_s", [4, 128, S]); bv_s = dscr("bv_s", [4, 128, NT * 130])
    akT_s = dscr("akT_s", [2, 128, S]); av_s = dscr("av_s", [4, 128, NT * 66])
    wkv_s = dscr("wkv_s", [128, 8 * 1536])

    with ExitStack() as es:
        BFN = 60 * 1024
        FN = 22400
        abf_t = es.enter_context(nc.sbuf_tensor("arena_bf", [128, BFN], BF16))
        af_t = es.enter_context(nc.sbuf_tensor("arena_f", [128, FN], F32))
        ABF = Arena(abf_t, BFN)
        AFL = Arena(af_t, FN)
        pairs = [es.enter_context(nc.psum_tensor(f"pair{i}", [128, 1024], F32)) for i in range(4)]
        pairs_bf = [p_[:].bitcast(BF16) for p_ in pairs]
        banks = [pairs[i // 2][:, (i % 2) * 512:(i % 2 + 1) * 512] for i in range(8)]
        banks_bf = [pairs_bf[i // 2][:, (i % 2) * 1024:(i % 2 + 1) * 1024] for i in range(8)]
        P = Prog(nc, es)
        block = es.enter_context(nc.Block())
        BK = [f"B{i}" for i in range(8)]

        def mm(out, lhsT, rhs, start, stop, r, w, skip=False):
            P.op("pe", lambda e: e.matmul(out, lhsT, rhs, start=start, stop=stop, skip_group_check=skip), r=r, w=w)

        def tr(out, in_, ident, r, w):
            P.op("pe", lambda e: e.transpose(out, in_, ident), r=r, w=w)

        def act(out, in_, func, bias, scale, r, w):
            P.op("act", lambda e: e.activation(out=out, in_=in_, func=func, bias=bias, scale=scale), r=r, w=w)

        def cp(eng, out, in_, r, w):
            if eng == "act":
                P.op(eng, lambda e: e.activation(out=out, in_=in_, func=AF.Copy), r=r, w=w)
            else:
                P.op(eng, lambda e: e.tensor_copy(out=out, in_=in_), r=r, w=w)

        def ts(eng, out, in0, s1, s2, op0, op1, r, w):
            if op1 is None:
                P.op(eng, lambda e: e.tensor_scalar(out=out, in0=in0, scalar1=s1, scalar2=None, op0=op0), r=r, w=w)
            else:
                P.op(eng, lambda e: e.tensor_scalar(out=out, in0=in0, scalar1=s1, scalar2=s2, op0=op0, op1=op1), r=r, w=w)

        def tt(eng, out, in0, in1, op, r, w):
            P.op(eng, lambda e: e.tensor_tensor(out=out, in0=in0, in1=in1, op=op), r=r, w=w)

        def stt(eng, out, in0, scalar, in1, op0, op1, r, w):
            P.op(eng, lambda e: e.scalar_tensor_tensor(out=out, in0=in0, scalar=scalar, in1=in1, op0=op0, op1=op1), r=r, w=w)

        def rsum(out, in_, r, w):
            P.op("dve", lambda e: e.reduce_sum(out=out, in_=in_, axis=AX.X), r=r, w=w)

        def recip(out, in_, r, w):
            P.op("dve", lambda e: e.reciprocal(out=out, in_=in_), r=r, w=w)

        def mset(eng, ap, val, w):
            P.op(eng, lambda e: e.memset(ap, val), w=w)

        def v3(ap, **kw):
            return ap.rearrange("p (a b) -> p a b", **kw)

        def v4(ap, **kw):
            return ap.rearrange("p (a b c) -> p a b c", **kw)

        ident = ABF.alloc(128)
        identf = AFL.alloc(128)
        mset("pool", identf, 1.0, ["identf"])
        P.op("pool", lambda e: e.affine_select(out=identf, in_=identf, pattern=[[-1, 128]], compare_op=ALU.is_equal,
                                                fill=0.0, base=0, channel_multiplier=1), r=["identf"], w=["identf"])
        cp("dve", ident, identf, ["identf"], ["ident"])
        cvec = AFL.alloc(16)
        P.dma("sp", cvec[:, 0:1], dbias[:, :], w=["cvec"])
        gsubt = AFL.alloc(128)
        P.dma("sp", gsubt, gsub.partition_broadcast(128), w=["gsubt"])
        ts("dve", gsubt, gsubt, 1.0 - LAMBDA_INIT, None, ALU.mult, None, ["gsubt"], ["gsubt"])
        lngt = AFL.alloc(1024); lnbt = AFL.alloc(1024)
        P.dma("sp", lngt, lng.partition_broadcast(128), w=["lngt"])
        P.dma("sp", lnbt, lnb.partition_broadcast(128), w=["lnbt"])
        lamt = AFL.alloc(256); lamp = AFL.alloc(128); lams = AFL.alloc(16)
        P.dma("sp", lamt, dlam.partition_broadcast(128), w=["lamt"])
        tt("dve", lamp[:, 0:64], lamt[:, 0:64], lamt[:, 64:128], ALU.mult, ["lamt"], ["lamp"])
        tt("dve", lamp[:, 64:128], lamt[:, 128:192], lamt[:, 192:256], ALU.mult, ["lamt"], ["lamp"])
        rsum(lams[:, 0:1], lamp[:, 0:64], ["lamp"], ["lams"])
        rsum(lams[:, 1:2], lamp[:, 64:128], ["lamp"], ["lams"])
        act(lams[:, 2:4], lams[:, 0:2], AF.Exp, 0.0, 1.0, ["lams"], ["lams2"])
        tt("dve", lams[:, 4:5], lams[:, 3:4], lams[:, 2:3], ALU.subtract, ["lams2"], ["lams3"])
        ts("dve", cvec[:, 1:2], lams[:, 4:5], -LAMBDA_INIT, None, ALU.add, None, ["lams3", "cvec"], ["cvec"])
        w_out_bf = v3(ABF.alloc(8 * 1024), a=8)
        wq = v3(ABF.alloc(8 * 2048), a=8)
        mkT = v3(ABF.alloc(2 * 256), a=2)
        mvaug = v4(ABF.alloc(2 * 4 * 66), a=2, b=4)
        bf_base = ABF.off
        f_base = AFL.off

        def rope(dst, src, cs, nparts, ngrp, r, w, tmpa, tmpb):
            wsrc = [k for k in w if k.startswith("B")]
            s3 = src.rearrange("p (g d) -> p g d", g=ngrp)
            d3 = dst.rearrange("p (g d) -> p g d", g=ngrp)
            ta = tmpa.rearrange("p (g d) -> p g d", g=ngrp)
            tb = tmpb.rearrange("p (g d) -> p g d", g=ngrp)
            cosb = cs[:, 0:32].unsqueeze(1).to_broadcast([nparts, ngrp, 32])
            sinb = cs[:, 32:64].unsqueeze(1).to_broadcast([nparts, ngrp, 32])
            x1 = s3[:, :, 0:32]; x2 = s3[:, :, 32:64]
            tt("dve", ta[:, :, 0:32], x1, cosb, ALU.mult, r, ["rt_a"] + wsrc)
            tt("dve", ta[:, :, 32:64], x2, cosb, ALU.mult, r, ["rt_a"] + wsrc)
            tt("dve", tb[:, :, 0:32], x2, sinb, ALU.mult, r, ["rt_b"] + wsrc)
            tt("dve", tb[:, :, 32:64], x1, sinb, ALU.mult, r, ["rt_b"] + wsrc)
            tt("dve", d3[:, :, 0:32], ta[:, :, 0:32], tb[:, :, 0:32], ALU.subtract, ["rt_a", "rt_b"], w)
            tt("dve", d3[:, :, 32:64], ta[:, :, 32:64], tb[:, :, 32:64], ALU.add, ["rt_a", "rt_b"], w)

        wm2 = v3(wq[:, 0:2, :].rearrange("p a b -> p (a b)"), a=8)
        memb = v3(wq[:, 2:3, :].rearrange("p a b -> p (a b)"), a=8)
        mkbf = wq[:, 3, 0:256]
        MEMK = ["wm2", "memb", "mkbf"]
        mkvf = AFL.alloc(512)
        mset("pool", mvaug[:, :, :, 64:66], 1.0, ["mvaug"])
        xta = [v3(ABF.alloc(8 * 512), a=8) for _ in range(2)]
        xT_all_v = xT_all.rearrange("(c p) s -> p c s", p=128)
        P.dma("pool", xta[0], xT_all_v[:, :, 0:512], w=["xta0"])

        wkv = v3(ABF.alloc(8 * 1536), a=8)
        w_in_v = w_in.rearrange("(c p) n -> p c n", p=128)
        P.dma("pool", wkv[:, :, 0:512], w_in_v[:, :, 256:768], w=["wkv0"])
        P.dma("pool", wkv[:, :, 512:1536], w_in_v[:, :, 1280:2304], w=["wkv1", "wkv2k"])
        P.dma("sp", wkv_s.rearrange("p (c n) -> p c n", c=8), wkv, r=["wkv0", "wkv1", "wkv2k"], w=["wkv_s"])
        cskv = v3(AFL.alloc(NT * 64), a=NT)
        P.dma("sp", cskv, cs_kv.rearrange("p (a b) -> p a b", a=NT), w=["cskv"])
        kst = [v3(ABF.alloc(4 * 512), a=4) for _ in range(2)]
        vst = [v4(ABF.alloc(4 * 4 * 130), a=4, b=4) for _ in range(2)]
        akst = [v3(ABF.alloc(2 * 512), a=2) for _ in range(2)]
        avst = [v4(ABF.alloc(4 * 4 * 66), a=4, b=4) for _ in range(2)]
        kbf = [ABF.alloc(768) for _ in range(2)]
        bkf = [AFL.alloc(512) for _ in range(2)]
        bvf = [AFL.alloc(512) for _ in range(2)]
        akvf = [AFL.alloc(512) for _ in range(2)]
        rta = AFL.alloc(512); rtb = AFL.alloc(512)
        for i in range(2):
            mset("pool", vst[i][:, :, :, 128:130], 1.0, [f"vst{i}"])
            mset("pool", avst[i][:, :, :, 64:66], 1.0, [f"avst{i}"])
        NG = NT // 4

        def mem_phase():
          for t in range(2):
              for c in range(8):
                  mm(banks[0][:, :], memb[:, c, t * 128:(t + 1) * 128], wm2[:, c, :], c == 0, c == 7, ["memb", "wm2"], [BK[0]])
              cp("dve", mkvf, banks[0][:, :], [], [BK[0], "mkvf"])
              P.dma("sp", mk_o[t * 128:(t + 1) * 128, :], mkvf[:, 0:256], r=["mkvf"], w=["o_mk"])
              P.dma("sp", mv_o[t * 128:(t + 1) * 128, :], mkvf[:, 256:512], r=["mkvf"], w=["o_mv"])
              cp("dve", mkbf, mkvf[:, 0:256], ["mkvf"], ["mkbf"])
              cp("dve", mvaug[:, t, :, 0:64], mkvf[:, 256:512].rearrange("p (h d) -> p h d", h=4), ["mkvf"], ["mvaug"])
              for j in range(2):
                  tr(banks_bf[1][:, j * 128:(j + 1) * 128], mkbf[:, j * 128:(j + 1) * 128], ident, ["mkbf", "ident"], [BK[1]])
              cp("dve", mkT[:, :, t * 128:(t + 1) * 128], banks_bf[1][:, 0:256].rearrange("p (a b) -> p a b", a=2), [], [BK[1], "mkT"])

        wq_loads = []
        wq_loads.append(lambda: P.dma("pool", wm2, w_mem.rearrange("(c p) n -> p c n", p=128), w=["wm2"]))
        wq_loads.append(lambda: P.dma("pool", memb, memT.rearrange("(c p) s -> p c s", p=128), w=["memb"]))
        n_mem_loads = len(wq_loads)
        for (lo, hi, slo, shi) in ((0, 256, 0, 256), (256, 512, 2304, 2560), (512, 1024, 768, 1280), (1024, 2048, 2560, 3584)):
            wq_loads.append(lambda lo=lo, hi=hi, slo=slo, shi=shi:
                            P.dma("pool", wq[:, :, lo:hi], w_in_v[:, :, slo:shi], w=["wq"] + MEMK))
        wq_loads.append(lambda: P.dma("pool", w_out_bf, w_out.rearrange("(c p) n -> p c n", p=128), w=["w_out_bf"]))
        mem_done = [False]

        def p1_mm(T):
            g, tl = T // 4, T % 4
            gb = g % 2
            if tl == 0 and g + 1 < NG:
                P.dma("pool", xta[(g + 1) % 2], xT_all_v[:, :, (g + 1) * 512:(g + 2) * 512], w=[f"xta{(g + 1) % 2}"])
                if g == 0:
                    for _ in range(n_mem_loads):
                        wq_loads.pop(0)()
                elif g >= 3:
                    if wq_loads:
                        wq_loads.pop(0)()
            pb = T % 2
            b0, b1, b2, b3 = (0, 1, 2, 3) if pb == 0 else (4, 5, 6, 7)
            for cb, bn in enumerate((b0, b1, b2)):
                for c in range(8):
                    mm(banks[bn][:, :], xta[gb][:, c, tl * 128:(tl + 1) * 128], wkv[:, c, cb * 512:(cb + 1) * 512],
                       c == 0, c == 7, [f"xta{gb}", "wkv%s" % ("2k" if cb == 2 else cb)], [BK[bn]])

        def p1_post(T):
            g, tl = T // 4, T % 4
            gb = g % 2
            pb = T % 2
            b0, b1, b2, b3 = (0, 1, 2, 3) if pb == 0 else (4, 5, 6, 7)
            cp("act", kbf[pb][:, 0:256], banks[b0][:, 0:256], [], [BK[b0], f"kbf{pb}"])
            cp("act", avst[gb][:, :, tl, 0:64], banks[b0][:, 256:512].rearrange("p (h d) -> p h d", h=4), [], [BK[b0], f"avst{gb}"])
            if T >= NT - 4:
                cp("dve", akvf[pb], banks[b0][:, :], [], [BK[b0], f"akvf{pb}"])
                r0 = (T - (NT - 4)) * 64
                P.dma("sp", ak_own[r0:r0 + 64, :], akvf[pb][0:64, 0:256], r=[f"akvf{pb}"], w=["o_ak"])
                P.dma("sp", av_own[r0:r0 + 64, :], akvf[pb][0:64, 256:512], r=[f"akvf{pb}"], w=["o_av"])
            rope(bkf[pb], banks[b1][:, :], cskv[:, T, :], 128, 8, ["cskv"], [BK[b1], f"bkf{pb}"], rta, rtb)
            P.dma("sp", bk_own[T * 64:(T + 1) * 64, :], bkf[pb][0:64, :], r=[f"bkf{pb}"], w=["o_bk"])
            cp("act", kbf[pb][:, 256:768], bkf[pb], [f"bkf{pb}"], [f"kbf{pb}"])
            cp("act", bvf[pb], banks[b2][:, :], [], [BK[b2], f"bvf{pb}"])
            P.dma("sp", bv_own[T * 64:(T + 1) * 64, :], bvf[pb][0:64, :], r=[f"bvf{pb}"], w=["o_bv"])
            cp("act", vst[gb][:, :, tl, 0:128], bvf[pb].rearrange("p (h d) -> p h d", h=4), [f"bvf{pb}"], [f"vst{gb}"])
            for j in range(6):
                tr(banks_bf[b3][:, j * 128:(j + 1) * 128], kbf[pb][:, j * 128:(j + 1) * 128], ident, [f"kbf{pb}", "ident"], [BK[b3]])
            cp("dve", akst[gb][:, :, tl * 128:(tl + 1) * 128], banks_bf[b3][:, 0:256].rearrange("p (a b) -> p a b", a=2), [], [BK[b3], f"akst{gb}"])
            cp("dve", kst[gb][:, :, tl * 128:(tl + 1) * 128], banks_bf[b3][:, 256:768].rearrange("p (a b) -> p a b", a=4), [], [BK[b3], f"kst{gb}"])
            if tl == 3:
                P.dma("sp", bkT_s.rearrange("h p s -> p h s")[:, :, g * 512:(g + 1) * 512], kst[gb], r=[f"kst{gb}"], w=["bkT_s"])
                P.dma("sp", akT_s.rearrange("h p s -> p h s")[:, :, g * 512:(g + 1) * 512], akst[gb], r=[f"akst{gb}"], w=["akT_s"])
                P.dma("sp", bv_s.rearrange("h p (t d) -> p h t d", d=130)[:, :, g * 4:(g + 1) * 4, :], vst[gb], r=[f"vst{gb}"], w=["bv_s"])
                P.dma("sp", av_s.rearrange("h p (t d) -> p h t d", d=66)[:, :, g * 4:(g + 1) * 4, :], avst[gb], r=[f"avst{gb}"], w=["av_s"])

        p1_mm(0)
        for T in range(NT):
            if T + 1 < NT:
                p1_mm(T + 1)
            p1_post(T)
            if T == 6:
                mem_phase()
        while wq_loads:
            wq_loads.pop(0)()

        P.barrier()
        ABF.off = bf_base; AFL.off = f_base
        P.op("pool", lambda e: e.memset(cvec[:, 8:9], 0.0), w=["cvec8"])
        abt = v4(AFL.alloc(4 * 5 * 64), a=4, b=5)
        P.dma("sp", abt, abias.rearrange("p (a b c) -> p a b c", a=4, b=5), w=["abt"])
        sabt = v4(AFL.alloc(4 * 5 * 16), a=4, b=5)
        P.dma("sp", sabt, sabias.rearrange("p (a b c) -> p a b c", a=4, b=5), w=["sabt"])
        css = AFL.alloc(64)
        P.dma("sp", css[0:16, :], cs_s[:, :], w=["css"])

        nt1 = AFL.alloc(128); nt2 = AFL.alloc(128)
        nsc = AFL.alloc(64)
        rta = AFL.alloc(512); rtb = AFL.alloc(512)
        gt4 = [AFL.alloc(1024) for _ in range(4)]
        xres4 = [AFL.alloc(1024) for _ in range(4)]
        zt4 = [AFL.alloc(1024) for _ in range(4)]
        xres, zt, yt, gtmp = xres4[0], zt4[0], gt4[1], gt4[0]
        qbf4 = [ABF.alloc(1024) for _ in range(4)]
        mgbf = qbf4[0]
        qT = ABF.alloc(4096)
        mgT4 = [v3(qT[:, u * 1024:(u + 1) * 1024], a=8) for u in range(4)]
        mgT = mgT4[0]
        bf_mark2 = ABF.off; f_mark2 = AFL.off
        aqT = v3(qT[:, 0:1024], a=2); mqT = v3(qT[:, 1024:2048], a=2); bqT = v3(qT[:, 2048:4096], a=4)
        xto = v3(ABF.alloc(8 * 512), a=8)
        NR = 3
        kseg = [ABF.alloc(1024) for _ in range(NR)]
        vseg = [v3(ABF.alloc(8 * 130), a=8) for _ in range(NR)]
        pT = [ABF.alloc(1024) for _ in range(3)]
        akw = v3(ABF.alloc(2 * 1536), a=2)
        avw = v4(ABF.alloc(4 * 12 * 66), a=4, b=12)
        apT = [ABF.alloc(320) for _ in range(8)]
        mpT = [ABF.alloc(512) for _ in range(4)]
        csqt = v3(AFL.alloc(4 * 64), a=4)
        mixed = v3(AFL.alloc(4 * 1024), a=4)

        def silu_gate(dst, src_psum, nparts, r, w):
            act(gtmp[0:nparts, :], src_psum, AF.Exp, 0.0, -1.0, r, w + ["gtmp"])
            ts("dve", gtmp[0:nparts, :], gtmp[0:nparts, :], 1.0, None, ALU.add, None, ["gtmp"], ["gtmp"])
            recip(gtmp[0:nparts, :], gtmp[0:nparts, :], ["gtmp"], ["gtmp"])
            tt("dve", dst, gtmp[0:nparts, :], src_psum, ALU.mult, ["gtmp"] + r, w + ["sgw"])

        def norm_plain(dst, o_ps, nparts, dv, r, w, p0=0, col=7):
            recip(nsc[p0:p0 + nparts, col:col + 1], o_ps[:, dv:dv + 1], r, w + [f"nsc{col}"])
            ts("dve", dst, o_ps[:, 0:dv], nsc[p0:p0 + nparts, col:col + 1], None, ALU.mult, None, [f"nsc{col}"] + r, w)

        def norm_diff(dst, o1, o2, nparts, r, w):
            recip(nsc[0:nparts, 0:1], o1[:, 128:129], r, w + ["nsc"])
            recip(nsc[0:nparts, 1:2], o2[:, 128:129], r, w + ["nsc1"])
            tt("dve", nsc[0:nparts, 2:3], nsc[0:nparts, 1:2], cvec[0:nparts, 1:2], ALU.mult, ["nsc1", "cvec"], ["nsc2"])
            ts("dve", nt1[0:nparts, :], o1[:, 0:128], nsc[0:nparts, 0:1], None, ALU.mult, None, ["nsc"] + r, w + ["nt1"])
            stt("dve", nt2[0:nparts, :], o2[:, 0:128], nsc[0:nparts, 2:3], nt1[0:nparts, :], ALU.mult, ALU.add, ["nsc2", "nt1"] + r, w + ["nt2"])
            tt("dve", nt1[0:nparts, :], nt2[0:nparts, :], nt2[0:nparts, :], ALU.mult, ["nt2"], ["nt1"])
            rsum(nsc[0:nparts, 3:4], nt1[0:nparts, :], ["nt1"], ["nsc3"])
            ts("dve", nsc[0:nparts, 4:5], nsc[0:nparts, 3:4], 1.0 / 128, 1e-5, ALU.mult, ALU.add, ["nsc3"], ["nsc4"])
            act(nsc[0:nparts, 5:6], nsc[0:nparts, 4:5], AF.Ln, 0.0, 1.0, ["nsc4"], ["nsc5"])
            act(nsc[0:nparts, 6:7], nsc[0:nparts, 5:6], AF.Exp, 0.0, -0.5, ["nsc5"], ["nsc6"])
            stt("dve", dst, nt2[0:nparts, :], nsc[0:nparts, 6:7], gsubt[0:nparts, :], ALU.mult, ALU.mult, ["nsc6", "nt2", "gsubt"], w)

        def out_phase(mix, sgt, xr_src, dst_dram, nparts, bA, bB, bT):
            P.dma("sp", xres[0:nparts, :], xr_src, w=["xres"])
            tt("dve", mgbf[0:nparts, :], mix, sgt, ALU.mult, ["mixed", "mixB", "sgw"], ["mgbf"])
            for j in range(8):
                tr(banks_bf[bT][:, j * 128:j * 128 + nparts], mgbf[0:nparts, j * 128:(j + 1) * 128], ident[0:nparts, 0:nparts],
                   ["mgbf", "ident"], [BK[bT]])
            cp("dve", mgT[:, :, 0:nparts], banks_bf[bT][:, 0:1024].rearrange("p (a b) -> p a b", a=8)[:, :, 0:nparts], [], [BK[bT], "mgT"])
            for cb, bn in enumerate((bA, bB)):
                for e_ in range(8):
                    mm(banks[bn][0:nparts, :], mgT[:, e_, 0:nparts], w_out_bf[:, e_, cb * 512:(cb + 1) * 512], e_ == 0, e_ == 7,
                       ["mgT", "w_out_bf"], [BK[bn]])
            for cb, bn in enumerate((bA, bB)):
                stt("dve", zt[0:nparts, cb * 512:(cb + 1) * 512], xres[0:nparts, cb * 512:(cb + 1) * 512], ALPHA, banks[bn][0:nparts, :],
                    ALU.mult, ALU.add, ["xres"], [BK[bn], "zt0", "zt0b"])
            rsum(nsc[0:nparts, 8:9], zt[0:nparts, :], ["zt0", "zt0b"], ["ln0"])
            ts("dve", nsc[0:nparts, 9:10], nsc[0:nparts, 8:9], -1.0 / 1024, None, ALU.mult, None, ["ln0"], ["ln1"])
            ts("dve", zt[0:nparts, :], zt[0:nparts, :], nsc[0:nparts, 9:10], None, ALU.add, None, ["ln1", "zt0", "zt0b"], ["zt0", "zt0b"])
            tt("dve", yt[0:nparts, :], zt[0:nparts, :], zt[0:nparts, :], ALU.mult, ["zt0", "zt0b"], ["yt"])
            rsum(nsc[0:nparts, 10:11], yt[0:nparts, :], ["yt"], ["ln2"])
            ts("dve", nsc[0:nparts, 11:12], nsc[0:nparts, 10:11], 1.0 / 1024, 1e-5, ALU.mult, ALU.add, ["ln2"], ["ln3"])
            act(nsc[0:nparts, 12:13], nsc[0:nparts, 11:12], AF.Ln, 0.0, 1.0, ["ln3"], ["ln4"])
            act(nsc[0:nparts, 13:14], nsc[0:nparts, 12:13], AF.Exp, 0.0, -0.5, ["ln4"], ["ln5"])
            stt("dve", yt[0:nparts, :], zt[0:nparts, :], nsc[0:nparts, 13:14], lngt[0:nparts, :], ALU.mult, ALU.mult, ["ln5", "zt0", "zt0b", "lngt"], ["yt"])
            tt("dve", yt[0:nparts, :], yt[0:nparts, :], lnbt[0:nparts, :], ALU.add, ["yt", "lnbt"], ["yt"])
            P.dma("sp", dst_dram, yt[0:nparts, :], r=["yt"], w=["o_y"])

        bkT_v = bkT_s
        bv_v = bv_s.rearrange("h p (t d) -> h p t d", d=130)
        akT_v = akT_s.rearrange("h p s -> p h s")
        av_v = av_s.rearrange("h p (t d) -> p h t d", d=66)
        xT_own_v = xT_own.rearrange("(c p) s -> p c s", p=128)
        cs_q_v = cs_q.rearrange("p (a b) -> p a b", b=64)
        OREG = [(4, 0), (4, 256), (5, 0), (5, 256), (6, 0), (6, 256), (7, 0), (7, 256)]

        def q_load(sp):
            P.dma("pool", xto, xT_own_v[:, :, sp * 512:(sp + 1) * 512], w=["xto"])
            P.dma("sp", csqt, cs_q_v[:, sp * 4:(sp + 1) * 4, :], w=["csqt"])

        def q_phase(sp):
            for u in range(4):
                for cb in range(2):
                    bn = 2 * u + cb
                    for c_ in range(8):
                        mm(banks[bn][:, :], xto[:, c_, u * 128:(u + 1) * 128], wq[:, c_, cb * 512:(cb + 1) * 512], c_ == 0, c_ == 7,
                           ["xto", "wq"], [BK[bn]])
            for u in range(4):
                cp("act", qbf4[u][:, 0:512], banks[2 * u][:, :], [], [BK[2 * u], f"qbf{u}"])
                rope(qbf4[u][:, 512:1024], banks[2 * u + 1][:, :], csqt[:, u, :], 128, 8, ["csqt"], [BK[2 * u + 1], f"qbf{u}"], rta, rtb)
            for u in range(4):
                for j in range(8):
                    tr(banks_bf[2 * u][:, j * 128:(j + 1) * 128], qbf4[u][:, j * 128:(j + 1) * 128], ident, [f"qbf{u}", "ident"], [BK[2 * u]])
            for u in range(4):
                us = slice(u * 128, (u + 1) * 128)
                cp("act", aqT[:, :, us], banks_bf[2 * u][:, 0:256].rearrange("p (a b) -> p a b", a=2), [], [BK[2 * u], f"aqT{u}"])
                cp("act", mqT[:, :, us], banks_bf[2 * u][:, 256:512].rearrange("p (a b) -> p a b", a=2), [], [BK[2 * u], f"mqT{u}"])
                cp("dve", bqT[:, :, us], banks_bf[2 * u][:, 512:1024].rearrange("p (a b) -> p a b", a=4), [], [BK[2 * u], f"bqT{u}"])
        QK_R = [f"aqT{u}" for u in range(4)]
        MQ_R = [f"mqT{u}" for u in range(4)]
        BQ_R = [f"bqT{u}" for u in range(4)]
        ALIAS = {0: QK_R, 1: MQ_R, 2: BQ_R, 3: BQ_R}

        def a_phase(sp):
            t_lo = max(0, sp * 8 - 4)
            nload = sp * 8 + 8 - t_lo
            off_t = t_lo - (sp * 8 - 4)
            P.dma("sp", akw[:, :, off_t * 128:(off_t + nload) * 128], akT_v[:, :, t_lo * 128:(t_lo + nload) * 128], r=["akT_s"], w=["akw"])
            P.dma("sp", avw[:, :, off_t:off_t + nload, :], av_v[:, :, t_lo:t_lo + nload, :], r=["av_s"], w=["avw"])

            def jbs_of(c_):
                C = sp * 8 + c_
                return [jb for jb in range(5) if C - 4 + jb >= 0]

            def qk(g):
                jbs = jbs_of(g)
                for hd in range(4):
                    bn = (g % 2) * 4 + hd
                    pr, hp = hd // 2, (hd % 2) * 64
                    for jb in jbs:
                        wt = g + jb
                        mm(banks[bn][:, jb * 64:(jb + 1) * 64], akw[hp:hp + 64, pr, wt * 128:(wt + 1) * 128],
                           aqT[hp:hp + 64, pr, g * 64:(g + 1) * 64], True, True, ["akw"] + QK_R, [BK[bn]])

            def pre(g):
                j0 = jbs_of(g)[0]
                for hd in range(4):
                    bn = (g % 2) * 4 + hd
                    stt("dve", banks[bn][:, j0 * 64:320], banks[bn][:, j0 * 64:320], 0.125,
                        abt[:, hd, j0:5, :].rearrange("p a b -> p (a b)"), ALU.mult, ALU.add, ["abt"], [BK[bn]])

            def ex(g):
                j0 = jbs_of(g)[0]
                for hd in range(4):
                    bn = (g % 2) * 4 + hd
                    act(apT[bn][:, j0 * 64:320], banks[bn][:, j0 * 64:320], AF.Exp, 0.0, 1.0, [], [BK[bn], f"apT{bn}"])

            def pv(g):
                jbs = jbs_of(g)
                p0 = 64 * (g % 2)
                for hd in range(4):
                    bn = (g % 2) * 4 + hd
                    ob = (g % 2) * 4 + (hd // 2) * 2
                    oc = 320 + (hd % 2) * 80
                    for jb in jbs:
                        wt = g + jb
                        mm(banks[ob][p0:p0 + 64, oc:oc + 65], apT[bn][:, jb * 64:(jb + 1) * 64], avw[:, hd, wt, 0:65],
                           jb == jbs[0] and hd % 2 == 0, jb == jbs[-1], [f"apT{bn}", "avw"], [BK[ob]], skip=True)

            def nrm(g):
                p0 = 64 * (g % 2)
                for hp_ in range(2):
                    ob = (g % 2) * 4 + hp_ * 2
                    o3 = banks[ob][p0:p0 + 64, 320:480].rearrange("p (h c) -> p h c", h=2)
                    scn = nsc[p0:p0 + 64, 16 + 2 * hp_:18 + 2 * hp_]
                    recip(scn, o3[:, :, 64], [], [BK[ob], f"nscA{hp_}"])
                    tt("dve", mixed[p0:p0 + 64, g // 2, hp_ * 128:(hp_ + 1) * 128].rearrange("p (h c) -> p h c", h=2), o3[:, :, 0:64],
                       scn.unsqueeze(2).to_broadcast([64, 2, 64]), ALU.mult, [f"nscA{hp_}"], [BK[ob], "mixA"])
            qk(0); pre(0); ex(0)
            for g in range(8):
                if g + 1 < 8:
                    qk(g + 1); pre(g + 1)
                pv(g)
                if g + 1 < 8:
                    ex(g + 1)
                nrm(g)

        def m_phase(sp):
            MREG = [0, 80, 160, 240]

            def qk(hh):
                for hd in (hh, hh + 1):
                    pr, hp = hd // 2, (hd % 2) * 64
                    for kt in range(2):
                        bn = (hh // 2) * 4 + (hd % 2) * 2 + kt
                        mm(banks[bn][:, :], mkT[hp:hp + 64, pr, kt * 128:(kt + 1) * 128], mqT[hp:hp + 64, pr, 0:512], True, True,
                           ["mkT"] + MQ_R, [BK[bn]])

            def ex(hh):
                for hd in (hh, hh + 1):
                    for kt in range(2):
                        bn = (hh // 2) * 4 + (hd % 2) * 2 + kt
                        act(mpT[(hd % 2) * 2 + kt][:, :], banks[bn][:, :], AF.Exp, 0.0, 0.125, [], [BK[bn], f"mpT{(hd % 2) * 2 + kt}"])

            def pv(hh):
                for hd in (hh, hh + 1):
                    bn = (hh // 2) * 4 + (hd % 2) * 2
                    for kt in range(2):
                        for qb in range(4):
                            col = MREG[qb]
                            mm(banks[bn][:, col:col + 65], mpT[(hd % 2) * 2 + kt][:, qb * 128:(qb + 1) * 128], mvaug[:, kt, hd, 0:65],
                               kt == 0 and qb == 0, kt == 1, [f"mpT{(hd % 2) * 2 + kt}", "mvaug"], [BK[bn]], skip=True)

            def nrm(hh):
                for hd in (hh, hh + 1):
                    bn = (hh // 2) * 4 + (hd % 2) * 2
                    o3 = banks[bn][:, 0:320].rearrange("p (q c) -> p q c", q=4)
                    scn = nsc[:, 20 + (hd % 2) * 4:24 + (hd % 2) * 4]
                    recip(scn, o3[:, :, 64], [], [BK[bn], f"nscM{hd % 2}"])
                    tt("dve", mixed[:, :, 768 + hd * 64:768 + (hd + 1) * 64], o3[:, :, 0:64], scn.unsqueeze(2).to_broadcast([128, 4, 64]),
                       ALU.mult, [f"nscM{hd % 2}"], [BK[bn], "mixM"])
            qk(0); ex(0); qk(2); pv(0); ex(2); nrm(0); pv(2); nrm(2)

        def out_phase_sp(sp):
            def tiles(g):
                return [(2 * g + t, 4 * g + 2 * t, 4 * g + 2 * t + 1) for t in range(2)]

            def s1(g):
                for u, b0, b1 in tiles(g):
                    U = sp * 4 + u
                    P.dma("sp", xres4[u], x_own[U * 128:(U + 1) * 128, :], w=[f"xres{u}"])
                    for cb, bn in ((0, b0), (1, b1)):
                        for c_ in range(8):
                            mm(banks[bn][:, :], xto[:, c_, u * 128:(u + 1) * 128], wq[:, c_, 1024 + cb * 512:1024 + (cb + 1) * 512],
                               c_ == 0, c_ == 7, ["xto", "wq"], [BK[bn]])

            def s2(g):
                for u, b0, b1 in tiles(g):
                    for cb, bn in ((0, b0), (1, b1)):
                        act(gt4[u][:, cb * 512:(cb + 1) * 512], banks[bn][:, :], AF.Exp, 0.0, -1.0, [], [BK[bn], f"gt{u}"])

            def s3(g):
                for u, b0, b1 in tiles(g):
                    act(gt4[u], gt4[u], AF.Ln, 1.0, 1.0, [f"gt{u}"], [f"gt{u}"])
                for u, b0, b1 in tiles(g):
                    act(gt4[u], gt4[u], AF.Exp, 0.0, -1.0, [f"gt{u}"], [f"gt{u}"])

            def s4(g):
                for u, b0, b1 in tiles(g):
                    for cb, bn in ((0, b0), (1, b1)):
                        tt("dve", gt4[u][:, cb * 512:(cb + 1) * 512], gt4[u][:, cb * 512:(cb + 1) * 512], banks[bn][:, :], ALU.mult,
                           [f"gt{u}"], [BK[bn], f"gt{u}"])
                    tt("dve", qbf4[u], mixed[:, u, :], gt4[u], ALU.mult, ["mixA", "mixB", "mixM", f"gt{u}"], [f"qbf{u}"])

            def s5(g):
                for u, b0, b1 in tiles(g):
                    for j in range(8):
                        tr(banks_bf[b0][:, j * 128:(j + 1) * 128], qbf4[u][:, j * 128:(j + 1) * 128], ident, [f"qbf{u}", "ident"], [BK[b0]])

            def s6(g):
                for u, b0, b1 in tiles(g):
                    cp("act", mgT4[u], banks_bf[b0][:, 0:1024].rearrange("p (a b) -> p a b", a=8), [], [BK[b0], f"mgT{u}"] + ALIAS[u])

            def s7(g):
                for u, b0, b1 in tiles(g):
                    for cb, bn in ((0, b0), (1, b1)):
                        for e_ in range(8):
                            mm(banks[bn][:, :], mgT4[u][:, e_, :], w_out_bf[:, e_, cb * 512:(cb + 1) * 512], e_ == 0, e_ == 7,
                               [f"mgT{u}", "w_out_bf"] + ALIAS[u], [BK[bn]])

            def s8(g):
                for u, b0, b1 in tiles(g):
                    for cb, bn in ((0, b0), (1, b1)):
                        stt("dve", zt4[u][:, cb * 512:(cb + 1) * 512], xres4[u][:, cb * 512:(cb + 1) * 512], ALPHA, banks[bn][:, :],
                            ALU.mult, ALU.add, [f"xres{u}"], [BK[bn], f"zt{u}"])
                    rsum(nsc[:, 32 + u:33 + u], zt4[u], [f"zt{u}"], [f"lna{u}"])

            def s9(g):
                for u, b0, b1 in tiles(g):
                    ts("dve", nsc[:, 32 + u:33 + u], nsc[:, 32 + u:33 + u], -1.0 / 1024, None, ALU.mult, None, [f"lna{u}"], [f"lna{u}"])
                    act(gt4[u], zt4[u], AF.Square, nsc[:, 32 + u:33 + u], 1.0, [f"lna{u}", f"zt{u}"], [f"gt{u}"])

            def s10(g):
                for u, b0, b1 in tiles(g):
                    rsum(nsc[:, 40 + u:41 + u], gt4[u], [f"gt{u}"], [f"lnb{u}"])
                    ts("dve", nsc[:, 40 + u:41 + u], nsc[:, 40 + u:41 + u], 1.0 / 1024, 1e-5, ALU.mult, ALU.add, [f"lnb{u}"], [f"lnb{u}"])

            def s11(g):
                for u, b0, b1 in tiles(g):
                    act(nsc[:, 40 + u:41 + u], nsc[:, 40 + u:41 + u], AF.Ln, 0.0, 1.0, [f"lnb{u}"], [f"lnb{u}"])
                    act(nsc[:, 40 + u:41 + u], nsc[:, 40 + u:41 + u], AF.Exp, 0.0, -0.5, [f"lnb{u}"], [f"lnb{u}"])

            def s12(g):
                for u, b0, b1 in tiles(g):
                    tt("dve", nsc[:, 32 + u:33 + u], nsc[:, 32 + u:33 + u], nsc[:, 40 + u:41 + u], ALU.mult, [f"lna{u}", f"lnb{u}"], [f"lna{u}"])
                    act(gt4[u], zt4[u], AF.Identity, nsc[:, 32 + u:33 + u], nsc[:, 40 + u:41 + u], [f"lna{u}", f"lnb{u}", f"zt{u}"], [f"gt{u}"])

            def s13(g):
                for u, b0, b1 in tiles(g):
                    U = sp * 4 + u
                    tt("dve", gt4[u], gt4[u], lngt, ALU.mult, [f"gt{u}", "lngt"], [f"gt{u}"])
                    tt("dve", zt4[u], gt4[u], lnbt, ALU.add, [f"gt{u}", "lnbt"], [f"zt{u}"])
                    P.dma("sp", y_own[U * 128:(U + 1) * 128, :], zt4[u], r=[f"zt{u}"], w=["o_y"])
            for stg in (s1, s2, s3, s4, s5, s6, s7, s8, s9, s10, s11, s12, s13):
                stg(0); stg(1)
                if stg is s1 and sp + 1 < nspan:
                    q_load(sp + 1)

        def norm_diff4(o1, o2, k1, k2, dst3, np_, split=False):
            t1 = zt4[0][0:np_, 0:512].rearrange("p (q c) -> p q c", q=4)
            t2 = zt4[0][0:np_, 512:1024].rearrange("p (q c) -> p q c", q=4)
            sq = zt4[1][0:np_, 0:512].rearrange("p (q c) -> p q c", q=4)
            sc = nsc[0:np_, 48:64]

            def bc(ap2):
                return ap2.unsqueeze(2).to_broadcast([np_, 4, 128])
            recip(sc[:, 0:4], o1[:, :, 128], [], k1 + ["nd_r1"])
            recip(sc[:, 4:8], o2[:, :, 128], [], k2 + ["nd_r2"])
            ts("dve", sc[:, 4:8], sc[:, 4:8], cvec[0:np_, 1:2], None, ALU.mult, None, ["nd_r2", "cvec"], ["nd_r2"])
            tt("dve", t1, o1[:, :, 0:128], bc(sc[:, 0:4]), ALU.mult, ["nd_r1"], k1 + ["zt0"])
            tt("dve", t2, o2[:, :, 0:128], bc(sc[:, 4:8]), ALU.mult, ["nd_r2"], k2 + ["zt0b"])
            tt("dve", t1, t1, t2, ALU.add, ["zt0", "zt0b"], ["zt0"])
            tt("dve", sq, t1, t1, ALU.mult, ["zt0"], ["zt1"])
            rsum(sc[:, 8:12], sq, ["zt1"], ["nd_ss"])
            ts("dve", sc[:, 8:12], sc[:, 8:12], 1.0 / 128, 1e-5, ALU.mult, ALU.add, ["nd_ss"], ["nd_ss"])

            def part2():
                act(sc[:, 12:16], sc[:, 8:12], AF.Ln, 0.0, 1.0, ["nd_ss"], ["nd_rs"])
                act(sc[:, 12:16], sc[:, 12:16], AF.Exp, 0.0, -0.5, ["nd_rs"], ["nd_rs"])
                tt("dve", t1, t1, bc(sc[:, 12:16]), ALU.mult, ["nd_rs", "zt0"], ["zt0"])
                tt("dve", dst3, t1, gsubt[0:np_, :].unsqueeze(1).to_broadcast([np_, 4, 128]), ALU.mult, ["zt0", "gsubt"], ["mixB"])
            if split:
                return part2
            part2()
            return None

        segctr = [0]

        b_slots = {}

        def b_load_seg(sp, j):
            segs = [(hd, s_) for hd in range(4) for s_ in range(sp + 1)]
            slot_of = b_slots.setdefault(sp, {})
            if j >= len(segs) or j in slot_of:
                return
            hd, s_ = segs[j]
            slot = segctr[0] % NR; segctr[0] += 1
            slot_of[j] = slot
            P.dma("sp", kseg[slot], bkT_v[hd, :, s_ * 1024:(s_ + 1) * 1024], r=["bkT_s"], w=[f"kseg{slot}"])
            P.dma("sp", vseg[slot][:, :, :], bv_v[hd, :, s_ * 8:(s_ + 1) * 8, :], r=["bv_s"], w=[f"vseg{slot}"])

        def b_phase(sp):
            slot_of = b_slots.setdefault(sp, {})

            def load_seg(j):
                b_load_seg(sp, j)

            pend = [None]
            for hd in range(4):
                tiles = [(s_, kt) for s_ in range(sp + 1) for kt in range(8)]
                nk = len(tiles)

                def front_qk(n):
                    s_, kt = tiles[n]
                    j = hd * (sp + 1) + s_
                    if kt == 0:
                        load_seg(j)
                        load_seg(j + 1)
                    slot = slot_of[j]
                    diag = (s_ == sp)
                    c0 = 64 * kt if diag else 0
                    pb_ = n % 2
                    for m in range(2):
                        bn = 2 * pb_ + m
                        mm(banks[bn][:, c0:512], kseg[slot][m * 64:(m + 1) * 64, kt * 128:(kt + 1) * 128],
                           bqT[m * 64:(m + 1) * 64, hd, c0:512], True, True, [f"kseg{slot}"] + BQ_R, [BK[bn]])

                def front_exp(n):
                    s_, kt = tiles[n]
                    diag = (s_ == sp)
                    c0 = 64 * kt if diag else 0
                    pb_ = n % 2
                    tb_ = n % 3
                    bk2 = [BK[2 * pb_], BK[2 * pb_ + 1]]
                    src3 = pairs[pb_][:, :].rearrange("p (m c) -> p m c", m=2)
                    dst3 = pT[tb_].rearrange("p (m c) -> p m c", m=2)
                    if diag:
                        act(dst3[:, :, c0:c0 + 64], src3[:, :, c0:c0 + 64], AF.Exp, cvec[:, 0:1], 0.125, ["cvec"], bk2 + [f"pT{tb_}"])
                        if c0 + 64 < 512:
                            act(dst3[:, :, c0 + 64:512], src3[:, :, c0 + 64:512], AF.Exp, 0.0, 0.125, [], bk2 + [f"pT{tb_}"])
                    else:
                        act(pT[tb_], pairs[pb_][:, :], AF.Exp, 0.0, 0.125, [], bk2 + [f"pT{tb_}"])

                def back(n):
                    s_, kt = tiles[n]
                    slot = slot_of[hd * (sp + 1) + s_]
                    diag = (s_ == sp)
                    tb_ = n % 3
                    qb0 = (kt // 2) if diag else 0
                    first_bank = set()
                    for m in range(2):
                        for qb in range(qb0, 4):
                            bn, col = OREG[m * 4 + qb]
                            st_ = (n == 0 and bn not in first_bank)
                            first_bank.add(bn)
                            last = (diag and kt == 2 * qb + 1)
                            if diag and kt % 2 == 1 and qb == qb0:
                                mm(banks[bn][64:128, col:col + 129], pT[tb_][:, m * 512 + qb * 128 + 64:m * 512 + (qb + 1) * 128],
                                   vseg[slot][:, kt, 0:129], st_, last, [f"pT{tb_}", f"vseg{slot}"], [BK[bn]], skip=True)
                            else:
                                mm(banks[bn][:, col:col + 129], pT[tb_][:, m * 512 + qb * 128:m * 512 + (qb + 1) * 128], vseg[slot][:, kt, 0:129],
                                   st_, last, [f"pT{tb_}", f"vseg{slot}"], [BK[bn]], skip=True)
                front_qk(0)
                for n in range(nk):
                    front_exp(n)
                    if n + 1 < nk:
                        front_qk(n + 1)
                    if n == 2 and pend[0] is not None:
                        pend[0](); pend[0] = None
                    if n > 0:
                        back(n - 1)
                back(nk - 1)
                if pend[0] is not None:
                    pend[0](); pend[0] = None
                pend[0] = norm_diff4(pairs[2][:, :].rearrange("p (q c) -> p q c", q=4), pairs[3][:, :].rearrange("p (q c) -> p q c", q=4),
                                     [BK[4], BK[5]], [BK[6], BK[7]], mixed[:, :, 256 + hd * 128:256 + (hd + 1) * 128], 128, split=True)
            if pend[0] is not None:
                pend[0](); pend[0] = None

        q_load(0)
        for sp_ in range(nspan):
            b_load_seg(sp_, 0); b_load_seg(sp_, 1)
            q_phase(sp_)
            b_phase(sp_)
            a_phase(sp_)
            m_phase(sp_)
            out_phase_sp(sp_)

        P.barrier()
        ABF.off = bf_mark2; AFL.off = f_mark2
        wkv2 = v3(ABF.alloc(8 * 1536), a=8)
        P.dma("sp", wkv2, wkv_s.rearrange("p (c n) -> p c n", c=8), w=["wkv2"])
        xsb = v3(ABF.alloc(8 * 64), a=8)
        P.dma("pool", xsb, xsT.rearrange("(c p) s -> p c s", p=128), w=["xsb"])
        qsb = ABF.alloc(1792)
        sT = v3(ABF.alloc(14 * 16), a=14)
        sbva = v3(ABF.alloc(4 * 130), a=4); sava = v3(ABF.alloc(4 * 66), a=4)
        ckb = v3(ABF.alloc(4 * 1024), a=4); cvb_f = ABF.alloc(8 * 4 * 130); cvb = v4(cvb_f, a=8, b=4)
        cka = v3(ABF.alloc(2 * 512), a=2); cva_f = ABF.alloc(4 * 4 * 66); cva = v4(cva_f, a=4, b=4)
        ckm = v3(ABF.alloc(2 * 256), a=2); cvm_f = ABF.alloc(2 * 4 * 66); cvm = v4(cvm_f, a=2, b=4)
        spTb = v4(qT[:, 1024:2176], a=4, b=2)
        spTa = v4(qT[:, 2176:2496], a=2, b=2)
        spTm = v3(qT[:, 2496:2624], a=2)
        sgs = AFL.alloc(1024); mixs = AFL.alloc(1024)
        skvf = AFL.alloc(512); sbkf = AFL.alloc(512); sbvf = AFL.alloc(512); sbqf = AFL.alloc(512)
        mset("pool", sbva[:, :, 128:130], 1.0, ["sbva"]); mset("pool", sava[:, :, 64:66], 1.0, ["sava"])
        mset("pool", cvb[:, :, :, 128:130], 1.0, ["cvb"]); mset("pool", cva[:, :, :, 64:66], 1.0, ["cva"])
        mset("pool", cvm[:, :, :, 64:66], 1.0, ["cvm"])
        for bb in range(4):
            P.dma("pool", ckb, cbkT[bb].rearrange("h p s -> p h s"), w=["ckb"])
            P.dma("pool", cvb_f.rearrange("p (g d) -> p g d", d=130)[:, :, 0:128], cbv[bb].rearrange("p (g d) -> p g d", d=128), w=["cvb"])
            P.dma("pool", cka, cakT[bb].rearrange("h p s -> p h s"), w=["cka"])
            P.dma("pool", cva_f.rearrange("p (g d) -> p g d", d=66)[:, :, 0:64], cav[bb].rearrange("p (g d) -> p g d", d=64), w=["cva"])
            P.dma("pool", ckm, cmkT[bb].rearrange("h p s -> p h s"), w=["ckm"])
            P.dma("pool", cvm_f.rearrange("p (g d) -> p g d", d=66)[:, :, 0:64], cmv[bb].rearrange("p (g d) -> p g d", d=64), w=["cvm"])
            tok = slice(bb * 16, (bb + 1) * 16)
            for cb in range(7):
                wsrc = wq if cb < 4 else wkv2
                cbo = cb if cb < 4 else cb - 4
                for c in range(8):
                    mm(banks[cb][0:16, :], xsb[:, c, tok], wsrc[:, c, cbo * 512:(cbo + 1) * 512], c == 0, c == 7,
                       ["xsb", "wq", "wkv2"], [BK[cb]])
            cp("act", qsb[0:16, 0:512], banks[0][0:16, :], [], [BK[0], "qsb"])
            rope(sbqf[0:16, :], banks[1][0:16, :], css[0:16, :], 16, 8, ["css"], [BK[1], "sbqf"], rta[0:16, :], rtb[0:16, :])
            cp("act", qsb[0:16, 512:1024], sbqf[0:16, :], ["sbqf"], ["qsb"])
            for half, bn in enumerate((2, 3)):
                hs = slice(half * 512, (half + 1) * 512)
                act(gtmp[0:16, hs], banks[bn][0:16, :], AF.Exp, 0.0, -1.0, [], [BK[bn], "gtmp"])
                act(gtmp[0:16, hs], gtmp[0:16, hs], AF.Ln, 1.0, 1.0, ["gtmp"], ["gtmp"])
                act(gtmp[0:16, hs], gtmp[0:16, hs], AF.Exp, 0.0, -1.0, ["gtmp"], ["gtmp"])
                tt("dve", sgs[0:16, hs], gtmp[0:16, hs], banks[bn][0:16, :], ALU.mult, ["gtmp"], [BK[bn], "sgw"])
            cp("dve", skvf[0:16, :], banks[4][0:16, :], [], [BK[4], "skvf"])
            P.dma("sp", sak[tok, :], skvf[0:16, 0:256], r=["skvf"], w=["o_sak"])
            P.dma("sp", sav[tok, :], skvf[0:16, 256:512], r=["skvf"], w=["o_sav"])
            cp("act", qsb[0:16, 1024:1280], skvf[0:16, 0:256], ["skvf"], ["qsb"])
            cp("dve", sava[0:16, :, 0:64], skvf[0:16, 256:512].rearrange("p (h d) -> p h d", h=4), ["skvf"], ["sava"])
            rope(sbkf[0:16, :], banks[5][0:16, :], css[0:16, :], 16, 8, ["css"], [BK[5], "sbkf"], rta[0:16, :], rtb[0:16, :])
            P.dma("sp", sbk[tok, :], sbkf[0:16, :], r=["sbkf"], w=["o_sbk"])
            cp("act", qsb[0:16, 1280:1792], sbkf[0:16, :], ["sbkf"], ["qsb"])
            cp("dve", sbvf[0:16, :], banks[6][0:16, :], [], [BK[6], "sbvf"])
            P.dma("sp", sbv[tok, :], sbvf[0:16, :], r=["sbvf"], w=["o_sbv"])
            cp("dve", sbva[0:16, :, 0:128], sbvf[0:16, :].rearrange("p (h d) -> p h d", h=4), ["sbvf"], ["sbva"])
            for j in range(14):
                tr(banks_bf[7][:, j * 16:(j + 1) * 16], qsb[0:16, j * 128:(j + 1) * 128], ident[0:16, 0:16], ["qsb", "ident"], [BK[7]])
            cp("dve", sT, banks_bf[7][:, 0:224].rearrange("p (a b) -> p a b", a=14), [], [BK[7], "sT"])
            for m in range(2):
                ms = slice(m * 64, (m + 1) * 64)
                for hd in range(4):
                    bn = 2 * m + hd // 2; c0 = (hd % 2) * 144
                    for kt in range(8):
                        mm(banks[bn][:, c0 + kt * 16:c0 + (kt + 1) * 16], ckb[ms, hd, kt * 128:(kt + 1) * 128], sT[ms, 4 + hd, :], True, True,
                           ["ckb", "sT"], [BK[bn]])
                    mm(banks[bn][0:16, c0 + 128:c0 + 144], sT[ms, 10 + hd, :], sT[ms, 4 + hd, :], True, True, ["sT"], [BK[bn]])
            for hd in range(4):
                pr, hp = hd // 2, (hd % 2) * 64
                hsl = slice(hp, hp + 64)
                bn = 4 + hd % 2; c0 = (hd // 2) * 80
                for kt in range(4):
                    mm(banks[bn][:, c0 + kt * 16:c0 + (kt + 1) * 16], cka[hsl, pr, kt * 128:(kt + 1) * 128], sT[hsl, 0 + pr, :], True, True,
                       ["cka", "sT"], [BK[bn]])
                mm(banks[bn][0:16, c0 + 64:c0 + 80], sT[hsl, 8 + pr, :], sT[hsl, 0 + pr, :], True, True, ["sT"], [BK[bn]])
                bn = 6 + hd % 2; c0 = (hd // 2) * 32
                for kt in range(2):
                    mm(banks[bn][:, c0 + kt * 16:c0 + (kt + 1) * 16], ckm[hsl, pr, kt * 128:(kt + 1) * 128], sT[hsl, 2 + pr, :], True, True,
                       ["ckm", "sT"], [BK[bn]])
            for bn in range(4):
                src3 = banks[bn][:, 0:288].rearrange("p (h c) -> p h c", h=2)
                act(spTb[:, bn, :, 0:128], src3[:, :, 0:128], AF.Exp, 0.0, 0.125, [], [BK[bn], "spTb"])
                act(spTb[0:16, bn, :, 128:144], src3[0:16, :, 128:144], AF.Exp, 0.0, 0.125, [], [BK[bn], "spTb"])
            for hd in range(4):
                bn = 4 + hd % 2; c0 = (hd // 2) * 80
                stt("dve", banks[bn][:, c0:c0 + 64], banks[bn][:, c0:c0 + 64], 0.125, sabt[:, hd, 0:4, :].rearrange("p a b -> p (a b)"),
                    ALU.mult, ALU.add, ["sabt"], [BK[bn]])
                stt("dve", banks[bn][0:16, c0 + 64:c0 + 80], banks[bn][0:16, c0 + 64:c0 + 80], 0.125, sabt[0:16, hd, 4, :],
                    ALU.mult, ALU.add, ["sabt"], [BK[bn]])
            for p_ in range(2):
                src3 = banks[4 + p_][:, 0:160].rearrange("p (h c) -> p h c", h=2)
                act(spTa[:, p_, :, 0:64], src3[:, :, 0:64], AF.Exp, 0.0, 1.0, [], [BK[4 + p_], "spTa"])
                act(spTa[0:16, p_, :, 64:80], src3[0:16, :, 64:80], AF.Exp, 0.0, 1.0, [], [BK[4 + p_], "spTa"])
                act(spTm[:, p_, :], banks[6 + p_][:, 0:64], AF.Exp, 0.0, 0.125, [], [BK[6 + p_], "spTm"])
            for m in range(2):
                for hd in range(4):
                    bn = 2 * m + hd // 2
                    ob = banks[bn][0:16, (hd % 2) * 256:(hd % 2) * 256 + 129]
                    for kt in range(8):
                        mm(ob, spTb[:, bn, hd % 2, kt * 16:(kt + 1) * 16], cvb[:, kt, hd, 0:129], kt == 0 and hd % 2 == 0, False,
                           ["spTb", "cvb"], [BK[bn]], skip=True)
                    mm(ob, spTb[0:16, bn, hd % 2, 128:144], sbva[0:16, hd, 0:129], False, True, ["spTb", "sbva"], [BK[bn]], skip=True)
            for hd in range(4):
                oa = banks[4][0:16, hd * 80:hd * 80 + 65]
                for kt in range(4):
                    mm(oa, spTa[:, hd % 2, hd // 2, kt * 16:(kt + 1) * 16], cva[:, kt, hd, 0:65], kt == 0 and hd == 0, False,
                       ["spTa", "cva"], [BK[4]], skip=True)
                mm(oa, spTa[0:16, hd % 2, hd // 2, 64:80], sava[0:16, hd, 0:65], False, True, ["spTa", "sava"], [BK[4]], skip=True)
                om = banks[6][0:16, hd * 80:hd * 80 + 65]
                for kt in range(2):
                    mm(om, spTm[:, hd % 2, (hd // 2) * 32 + kt * 16:(hd // 2) * 32 + (kt + 1) * 16], cvm[:, kt, hd, 0:65],
                       kt == 0 and hd == 0, kt == 1, ["spTm", "cvm"], [BK[6]], skip=True)
            norm_diff4(pairs[0][0:16, :].rearrange("p (q c) -> p q c", q=4), pairs[1][0:16, :].rearrange("p (q c) -> p q c", q=4),
                       [BK[0], BK[1]], [BK[2], BK[3]], mixs[0:16, 256:768].rearrange("p (q c) -> p q c", q=4), 16)
            for bn, c_lo in ((4, 0), (6, 768)):
                o3 = banks[bn][0:16, 0:320].rearrange("p (h c) -> p h c", h=4)
                recip(nsc[0:16, 20:24], o3[:, :, 64], [], [BK[bn], "sn_r"])
                tt("dve", mixs[0:16, c_lo:c_lo + 256].rearrange("p (h c) -> p h c", h=4), o3[:, :, 0:64],
                   nsc[0:16, 20:24].unsqueeze(2).to_broadcast([16, 4, 64]), ALU.mult, ["sn_r"], [BK[bn], "mixed"])
            out_phase(mixs[0:16, :], sgs[0:16, :], xs[tok, :], ys[tok, :], 16, 0, 1, 2)
        P.emit(block)
    return nc


def _rope_tables(pos):
    inv = (10000.0 ** (-np.arange(32, dtype=np.float64) / 32.0)).astype(np.float32)
    ang = (np.asarray(pos).astype(np.float32)[..., None] * inv).astype(np.float32).astype(np.float64)
    return np.concatenate([np.cos(ang), np.sin(ang)], -1).astype(np.float32)


def _core_inputs(inp, core, nspan):
    b, h = core // 2, core % 2
    S = 1024 * nspan
    NT = S // 128
    xp = inp["x_prompt"][b]
    T = np.arange(NT)
    own = 2 * T + h
    oth = 2 * T + 1 - h
    idx_all = (np.stack([own, oth], 1)[:, :, None] * 64 + np.arange(64)).reshape(-1)
    idx_own = (own[:, None] * 64 + np.arange(64)).reshape(-1)
    d = {}
    d["xT_all"] = np.ascontiguousarray(xp[idx_all].T)
    d["xT_own"] = np.ascontiguousarray(xp[idx_own].T)
    d["x_own"] = np.ascontiguousarray(xp[idx_own])
    d["memT"] = np.ascontiguousarray(inp["mem_prompt"][b].T)
    d["w_in"] = np.ascontiguousarray(inp["w_in"][0]); d["w_mem"] = np.ascontiguousarray(inp["w_mem_kv"][0])
    d["w_out"] = np.ascontiguousarray(inp["w_out"][0])
    d["cs_kv"] = np.ascontiguousarray(_rope_tables(idx_all.reshape(NT, 128)).transpose(1, 0, 2).reshape(128, NT * 64))
    NU = NT // 2
    d["cs_q"] = np.ascontiguousarray(_rope_tables(idx_own.reshape(NU, 128)).transpose(1, 0, 2).reshape(128, NU * 64))
    past = inp["cache_b_k"].shape[2]
    d["cs_s"] = _rope_tables(past + np.arange(16))
    table = inp["a_rel_bias"][0]
    kk = np.arange(128)
    perm = np.where(kk < 64, 64 * h + kk, 64 * (1 - h) + kk - 64)
    q = np.arange(64)
    ab = np.empty((128, 4, 5, 64), np.float32)
    for jb in range(5):
        dist = 512 - 128 * jb + 64 * h + q[None, :] - perm[:, None]
        kchunk_rel = 2 * (jb - 4) + perm // 64 - h
        valid = (kchunk_rel >= -8) & (kchunk_rel <= 0)
        g = table[:, np.clip(dist, -128, 128) + 128]
        g = np.where(valid[None, :, None], g, np.float32(NEG))
        ab[:, :, jb, :] = g.transpose(1, 0, 2)
    d["abias"] = ab.reshape(128, -1)
    sab = np.zeros((128, 4, 5, 16), np.float32)
    t = np.arange(16)
    for jb in range(5):
        ki = 128 * jb + kk if jb < 4 else 512 + np.minimum(kk, 15)
        dist = t[None, :] + 512 - ki[:, None]
        g = table[:, np.clip(dist, -128, 128) + 128]
        sab[:, :, jb, :] = g.transpose(1, 0, 2)
    d["sabias"] = sab.reshape(128, -1)
    db = np.zeros((128, 1), np.float32)
    if h == 0:
        db[64:] = NEG
    d["dbias"] = db
    d["dlam"] = np.ascontiguousarray(inp["diff_lambda"][0].reshape(1, 256))
    d["gsub"] = np.ascontiguousarray(inp["diff_subln_g"][0].reshape(1, 128))
    d["lng"] = np.ascontiguousarray(inp["ln_g"][0].reshape(1, 1024)); d["lnb"] = np.ascontiguousarray(inp["ln_b"][0].reshape(1, 1024))
    sb = slice(4 * core, 4 * core + 4)
    xs = inp["x_sample"][sb].reshape(64, 1024)
    d["xsT"] = np.ascontiguousarray(xs.T); d["xs"] = np.ascontiguousarray(xs)
    d["cbkT"] = np.ascontiguousarray(inp["cache_b_k"][0, sb].reshape(4, -1, 4, 128).transpose(0, 2, 3, 1))
    d["cbv"] = np.ascontiguousarray(inp["cache_b_v"][0, sb].reshape(4, -1, 128, 512).transpose(0, 2, 1, 3).reshape(4, 128, -1))
    d["cakT"] = np.ascontiguousarray(inp["cache_a_k"][0, sb].reshape(4, -1, 2, 128).transpose(0, 2, 3, 1))
    d["cav"] = np.ascontiguousarray(inp["cache_a_v"][0, sb].reshape(4, -1, 128, 256).transpose(0, 2, 1, 3).reshape(4, 128, -1))
    d["cmkT"] = np.ascontiguousarray(inp["cache_mem_k"][0, sb].reshape(4, -1, 2, 128).transpose(0, 2, 3, 1))
    d["cmv"] = np.ascontiguousarray(inp["cache_mem_v"][0, sb].reshape(4, -1, 128, 256).transpose(0, 2, 1, 3).reshape(4, 128, -1))
    return {k: np.ascontiguousarray(v, dtype=np.float32) for k, v in d.items()}


def _assemble(res, B, S, DB):
    NT = S // 128
    y = np.empty((B, S, 1024), np.float32); bk = np.empty((B, S, 512), np.float32); bv = np.empty((B, S, 512), np.float32)
    ak = np.empty((B, 512, 256), np.float32); av = np.empty((B, 512, 256), np.float32)
    mk = np.empty((B, 256, 256), np.float32); mv = np.empty((B, 256, 256), np.float32)
    ysm = np.empty((DB, 16, 1024), np.float32)
    sak = np.empty((DB, 16, 256), np.float32); sav = np.empty((DB, 16, 256), np.float32)
    sbk = np.empty((DB, 16, 512), np.float32); sbv = np.empty((DB, 16, 512), np.float32)
    for core, r in enumerate(res):
        b, h = core // 2, core % 2
        own = 2 * np.arange(NT) + h
        idx_own = (own[:, None] * 64 + np.arange(64)).reshape(-1)
        y[b, idx_own] = r["y_own"]; bk[b, idx_own] = r["bk_own"]; bv[b, idx_own] = r["bv_own"]
        last = idx_own[-256:] - (S - 512)
        ak[b, last] = r["ak_own"]; av[b, last] = r["av_own"]
        if h == 0:
            mk[b] = r["mk_o"]; mv[b] = r["mv_o"]
        sb = slice(4 * core, 4 * core + 4)
        ysm[sb] = r["ys"].reshape(4, 16, 1024)
        sak[sb] = r["sak"].reshape(4, 16, 256); sav[sb] = r["sav"].reshape(4, 16, 256)
        sbk[sb] = r["sbk"].reshape(4, 16, 512); sbv[sb] = r["sbv"].reshape(4, 16, 512)
    return (y, ysm, ak.reshape(1, B, 512, 4, 64), av.reshape(1, B, 512, 4, 64), bk.reshape(1, B, S, 4, 2, 64),
            bv.reshape(1, B, S, 4, 128), mk.reshape(1, B, 256, 4, 64), mv.reshape(1, B, 256, 4, 64),
            sak.reshape(1, DB, 16, 4, 64), sav.reshape(1, DB, 16, 4, 64), sbk.reshape(1, DB, 16, 4, 2, 64),
            sbv.reshape(1, DB, 16, 4, 128))


def kernel(**inputs):
    inp = {k: np.asarray(v) for k, v in inputs.items()}
    B, S, _ = inp["x_prompt"].shape
    DB = inp["x_sample"].shape[0]
    nspan = S // 1024
    nc = build(nspan)
    in_maps = [_core_inputs(inp, c, nspan) for c in range(N_CORES)]
    res = run_bass_kernel_spmd(nc, in_maps, core_ids=list(range(N_CORES)))
    return _assemble(res.results, B, S, DB)
```

```python
import numpy as np
from contextlib import ExitStack
import concourse.bass as bass
import concourse.mybir as mybir
from concourse.bass_utils import run_bass_kernel_spmd

F32 = mybir.dt.float32
BF16 = mybir.dt.bfloat16
AF = mybir.ActivationFunctionType
ALU = mybir.AluOpType
AX = mybir.AxisListType
NEG = -30000.0
ALPHA = 2.0 ** 0.25
LAMBDA_INIT = 0.8 - 0.6
N_CORES = 8


class Op:
    __slots__ = ("eng", "fn", "deps", "signal", "is_dma", "dsem", "dval", "prewait", "cnt")

    def __init__(self, eng, fn):
        self.eng = eng; self.fn = fn; self.deps = []; self.signal = False
        self.is_dma = False; self.dsem = None; self.dval = 0; self.prewait = None; self.cnt = 0


class Prog:
    ND = 8

    def __init__(self, nc, es):
        self.nc = nc
        self.engs = ("pe", "act", "dve", "pool", "sp")
        self.ops = {e: [] for e in self.engs}
        self.res = {}
        self.sems = {e: es.enter_context(nc.semaphore("s_" + e)) for e in self.engs}
        self.dsems = {q: [es.enter_context(nc.semaphore(f"d_{q}{i}")) for i in range(self.ND)] for q in ("sp", "pool")}
        self.dcount = {"sp": 0, "pool": 0}
        self.dmas = {"sp": [], "pool": []}
        self.bar = []

    def op(self, eng, fn, r=(), w=()):
        o = Op(eng, fn)
        deps = set(self.bar)
        for k in r:
            e = self.res.get(k)
            if e is not None and e[0] is not None:
                deps.add(e[0])
        for k in w:
            e = self.res.setdefault(k, [None, []])
            if e[0] is not None:
                deps.add(e[0])
            deps.update(e[1])
        for k in r:
            e = self.res.setdefault(k, [None, []])
            e[1] = [x for x in e[1] if x.is_dma or x.eng != eng]
            e[1].append(o)
        for k in w:
            e = self.res[k]
            e[0] = o; e[1] = []
        o.deps = [d for d in deps if d.is_dma or not (d.eng == "pe" and eng == "pe")]
        for d in o.deps:
            if not d.is_dma:
                d.signal = True
        self.ops[eng].append(o)
        return o

    def dma(self, q, out, in_, r=(), w=()):
        o = self.op(q, lambda e: e.dma_start(out=out, in_=in_), r=r, w=w)
        o.is_dma = True
        k = self.dcount[q]; self.dcount[q] += 1
        o.dsem = self.dsems[q][k % self.ND]
        o.dval = 16 * (k // self.ND + 1)
        if k >= self.ND:
            o.prewait = (o.dsem, 16 * (k // self.ND))
        self.dmas[q].append(o)
        return o

    def barrier(self):
        b = []
        for e in self.engs:
            for o in reversed(self.ops[e]):
                if not o.is_dma:
                    o.signal = True
                    b.append(o)
                    break
        for q in ("sp", "pool"):
            b.extend(self.dmas[q][-self.ND:])
        self.bar = b
        self.res = {}

    def emit(self, block):
        for e, lst in self.ops.items():
            c = 0
            for o in lst:
                if o.signal and not o.is_dma:
                    c += 1
                o.cnt = c

        def run(eng_name):
            def body(e):
                waited = {}

                def w(sem, val):
                    key = id(sem)
                    if waited.get(key, 0) < val:
                        e.wait_ge(sem, val); waited[key] = val
                for o in self.ops[eng_name]:
                    for d in o.deps:
                        if d.is_dma:
                            w(d.dsem, d.dval)
                        else:
                            w(self.sems[d.eng], d.cnt)
                    if o.is_dma and o.prewait is not None:
                        w(*o.prewait)
                    ins = o.fn(e)
                    if o.is_dma:
                        ins.then_inc(o.dsem, 16)
                    elif o.signal:
                        ins.then_inc(self.sems[o.eng], 1)
                if eng_name == "sp":
                    for q in ("sp", "pool"):
                        n = self.dcount[q]
                        for j in range(self.ND):
                            cntj = (n - j + self.ND - 1) // self.ND
                            if cntj > 0:
                                w(self.dsems[q][j], 16 * cntj)
            return body
        block.tensor(run("pe"))
        block.scalar(run("act"))
        block.vector(run("dve"))
        block.gpsimd(run("pool"))
        block.sync(run("sp"))


class Arena:
    def __init__(self, ap, size):
        self.ap = ap; self.size = size; self.off = 0

    def reset(self):
        self.off = 0

    def alloc(self, n):
        n = (n + 15) // 16 * 16
        assert self.off + n <= self.size, ("arena overflow", self.off, n, self.size)
        a = self.ap[:, self.off:self.off + n]
        self.off += n
        return a


def build(nspan):
    S = 1024 * nspan
    NT = S // 128
    NOWN = S // 2
    NU = NOWN // 128
    nc = bass.Bass("TRN2", target_bir_lowering=False)

    def din(name, shape):
        return nc.dram_tensor(name, list(shape), F32, kind="ExternalInput").ap()

    def dout(name, shape):
        return nc.dram_tensor(name, list(shape), F32, kind="ExternalOutput").ap()

    def dscr(name, shape):
        return nc.dram_tensor(name, list(shape), BF16, kind="Internal").ap()

    xT_all = din("xT_all", [1024, S]); xT_own = din("xT_own", [1024, NOWN]); x_own = din("x_own", [NOWN, 1024])
    memT = din("memT", [1024, 256])
    w_in = din("w_in", [1024, 3584]); w_mem = din("w_mem", [1024, 512]); w_out = din("w_out", [1024, 1024])
    cs_kv = din("cs_kv", [128, NT * 64]); cs_q = din("cs_q", [128, NU * 64]); cs_s = din("cs_s", [16, 64])
    abias = din("abias", [128, 4 * 5 * 64]); sabias = din("sabias", [128, 4 * 5 * 16])
    dbias = din("dbias", [128, 1])
    dlam = din("dlam", [1, 256]); gsub = din("gsub", [1, 128]); lng = din("lng", [1, 1024]); lnb = din("lnb", [1, 1024])
    xsT = din("xsT", [1024, 64]); xs = din("xs", [64, 1024])
    cbkT = din("cbkT", [4, 4, 128, 1024]); cbv = din("cbv", [4, 128, 4096])
    cakT = din("cakT", [4, 2, 128, 512]); cav = din("cav", [4, 128, 1024])
    cmkT = din("cmkT", [4, 2, 128, 256]); cmv = din("cmv", [4, 128, 512])

    y_own = dout("y_own", [NOWN, 1024]); ys = dout("ys", [64, 1024])
    bk_own = dout("bk_own", [NOWN, 512]); bv_own = dout("bv_own", [NOWN, 512])
    ak_own = dout("ak_own", [256, 256]); av_own = dout("av_own", [256, 256])
    mk_o = dout("mk_o", [256, 256]); mv_o = dout("mv_o", [256, 256])
    sak = dout("sak", [64, 256]); sav = dout("sav", [64, 256]); sbk = dout("sbk", [64, 512]); sbv = dout("sbv", [64, 512])

    bkT_s = dscr("bkT_s", [4, 128, S]); bv_s = dscr("bv_s", [4, 128, NT * 130])
    akT_s = dscr("akT_s", [2, 128, S]); av_s = dscr("av_s", [4, 128, NT * 66])
    wkv_s = dscr("wkv_s", [128, 8 * 1536])

    with ExitStack() as es:
        BFN = 60 * 1024
        FN = 22400
        abf_t = es.enter_context(nc.sbuf_tensor("arena_bf", [128, BFN], BF16))
        af_t = es.enter_context(nc.sbuf_tensor("arena_f", [128, FN], F32))
        ABF = Arena(abf_t, BFN)
        AFL = Arena(af_t, FN)
        pairs = [es.enter_context(nc.psum_tensor(f"pair{i}", [128, 1024], F32)) for i in range(4)]
        pairs_bf = [p_[:].bitcast(BF16) for p_ in pairs]
        banks = [pairs[i // 2][:, (i % 2) * 512:(i % 2 + 1) * 512] for i in range(8)]
        banks_bf = [pairs_bf[i // 2][:, (i % 2) * 1024:(i % 2 + 1) * 1024] for i in range(8)]
        P = Prog(nc, es)
        block = es.enter_context(nc.Block())
        BK = [f"B{i}" for i in range(8)]

        def mm(out, lhsT, rhs, start, stop, r, w, skip=False):
            P.op("pe", lambda e: e.matmul(out, lhsT, rhs, start=start, stop=stop, skip_group_check=skip), r=r, w=w)

        def tr(out, in_, ident, r, w):
            P.op("pe", lambda e: e.transpose(out, in_, ident), r=r, w=w)

        def act(out, in_, func, bias, scale, r, w):
            P.op("act", lambda e: e.activation(out=out, in_=in_, func=func, bias=bias, scale=scale), r=r, w=w)

        def cp(eng, out, in_, r, w):
            if eng == "act":
                P.op(eng, lambda e: e.activation(out=out, in_=in_, func=AF.Copy), r=r, w=w)
            else:
                P.op(eng, lambda e: e.tensor_copy(out=out, in_=in_), r=r, w=w)

        def ts(eng, out, in0, s1, s2, op0, op1, r, w):
            if op1 is None:
                P.op(eng, lambda e: e.tensor_scalar(out=out, in0=in0, scalar1=s1, scalar2=None, op0=op0), r=r, w=w)
            else:
                P.op(eng, lambda e: e.tensor_scalar(out=out, in0=in0, scalar1=s1, scalar2=s2, op0=op0, op1=op1), r=r, w=w)

        def tt(eng, out, in0, in1, op, r, w):
            P.op(eng, lambda e: e.tensor_tensor(out=out, in0=in0, in1=in1, op=op), r=r, w=w)

        def stt(eng, out, in0, scalar, in1, op0, op1, r, w):
            P.op(eng, lambda e: e.scalar_tensor_tensor(out=out, in0=in0, scalar=scalar, in1=in1, op0=op0, op1=op1), r=r, w=w)

        def rsum(out, in_, r, w):
            P.op("dve", lambda e: e.reduce_sum(out=out, in_=in_, axis=AX.X), r=r, w=w)

        def recip(out, in_, r, w):
            P.op("dve", lambda e: e.reciprocal(out=out, in_=in_), r=r, w=w)

        def mset(eng, ap, val, w):
            P.op(eng, lambda e: e.memset(ap, val), w=w)

        def v3(ap, **kw):
            return ap.rearrange("p (a b) -> p a b", **kw)

        def v4(ap, **kw):
            return ap.rearrange("p (a b c) -> p a b c", **kw)

        ident = ABF.alloc(128)
        identf = AFL.alloc(128)
        mset("pool", identf, 1.0, ["identf"])
        P.op("pool", lambda e: e.affine_select(out=identf, in_=identf, pattern=[[-1, 128]], compare_op=ALU.is_equal,
                                                fill=0.0, base=0, channel_multiplier=1), r=["identf"], w=["identf"])
        cp("dve", ident, identf, ["identf"], ["ident"])
        cvec = AFL.alloc(16)
        P.dma("sp", cvec[:, 0:1], dbias[:, :], w=["cvec"])
        gsubt = AFL.alloc(128)
        P.dma("sp", gsubt, gsub.partition_broadcast(128), w=["gsubt"])
        ts("dve", gsubt, gsubt, 1.0 - LAMBDA_INIT, None, ALU.mult, None, ["gsubt"], ["gsubt"])
        lngt = AFL.alloc(1024); lnbt = AFL.alloc(1024)
        P.dma("sp", lngt, lng.partition_broadcast(128), w=["lngt"])
        P.dma("sp", lnbt, lnb.partition_broadcast(128), w=["lnbt"])
        lamt = AFL.alloc(256); lamp = AFL.alloc(128); lams = AFL.alloc(16)
        P.dma("sp", lamt, dlam.partition_broadcast(128), w=["lamt"])
        tt("dve", lamp[:, 0:64], lamt[:, 0:64], lamt[:, 64:128], ALU.mult, ["lamt"], ["lamp"])
        tt("dve", lamp[:, 64:128], lamt[:, 128:192], lamt[:, 192:256], ALU.mult, ["lamt"], ["lamp"])
        rsum(lams[:, 0:1], lamp[:, 0:64], ["lamp"], ["lams"])
        rsum(lams[:, 1:2], lamp[:, 64:128], ["lamp"], ["lams"])
        act(lams[:, 2:4], lams[:, 0:2], AF.Exp, 0.0, 1.0, ["lams"], ["lams2"])
        tt("dve", lams[:, 4:5], lams[:, 3:4], lams[:, 2:3], ALU.subtract, ["lams2"], ["lams3"])
        ts("dve", cvec[:, 1:2], lams[:, 4:5], -LAMBDA_INIT, None, ALU.add, None, ["lams3", "cvec"], ["cvec"])
        w_out_bf = v3(ABF.alloc(8 * 1024), a=8)
        wq = v3(ABF.alloc(8 * 2048), a=8)
        mkT = v3(ABF.alloc(2 * 256), a=2)
        mvaug = v4(ABF.alloc(2 * 4 * 66), a=2, b=4)
        bf_base = ABF.off
        f_base = AFL.off

        def rope(dst, src, cs, nparts, ngrp, r, w, tmpa, tmpb):
            wsrc = [k for k in w if k.startswith("B")]
            s3 = src.rearrange("p (g d) -> p g d", g=ngrp)
            d3 = dst.rearrange("p (g d) -> p g d", g=ngrp)
            ta = tmpa.rearrange("p (g d) -> p g d", g=ngrp)
            tb = tmpb.rearrange("p (g d) -> p g d", g=ngrp)
            cosb = cs[:, 0:32].unsqueeze(1).to_broadcast([nparts, ngrp, 32])
            sinb = cs[:, 32:64].unsqueeze(1).to_broadcast([nparts, ngrp, 32])
            x1 = s3[:, :, 0:32]; x2 = s3[:, :, 32:64]
            tt("dve", ta[:, :, 0:32], x1, cosb, ALU.mult, r, ["rt_a"] + wsrc)
            tt("dve", ta[:, :, 32:64], x2, cosb, ALU.mult, r, ["rt_a"] + wsrc)
            tt("dve", tb[:, :, 0:32], x2, sinb, ALU.mult, r, ["rt_b"] + wsrc)
            tt("dve", tb[:, :, 32:64], x1, sinb, ALU.mult, r, ["rt_b"] + wsrc)
            tt("dve", d3[:, :, 0:32], ta[:, :, 0:32], tb[:, :, 0:32], ALU.subtract, ["rt_a", "rt_b"], w)
            tt("dve", d3[:, :, 32:64], ta[:, :, 32:64], tb[:, :, 32:64], ALU.add, ["rt_a", "rt_b"], w)

        wm2 = v3(wq[:, 0:2, :].rearrange("p a b -> p (a b)"), a=8)
        memb = v3(wq[:, 2:3, :].rearrange("p a b -> p (a b)"), a=8)
        mkbf = wq[:, 3, 0:256]
        MEMK = ["wm2", "memb", "mkbf"]
        mkvf = AFL.alloc(512)
        mset("pool", mvaug[:, :, :, 64:66], 1.0, ["mvaug"])
        xta = [v3(ABF.alloc(8 * 512), a=8) for _ in range(2)]
        xT_all_v = xT_all.rearrange("(c p) s -> p c s", p=128)
        P.dma("pool", xta[0], xT_all_v[:, :, 0:512], w=["xta0"])

        wkv = v3(ABF.alloc(8 * 1536), a=8)
        w_in_v = w_in.rearrange("(c p) n -> p c n", p=128)
        P.dma("pool", wkv[:, :, 0:512], w_in_v[:, :, 256:768], w=["wkv0"])
        P.dma("pool", wkv[:, :, 512:1536], w_in_v[:, :, 1280:2304], w=["wkv1", "wkv2k"])
        P.dma("sp", wkv_s.rearrange("p (c n) -> p c n", c=8), wkv, r=["wkv0", "wkv1", "wkv2k"], w=["wkv_s"])
        cskv = v3(AFL.alloc(NT * 64), a=NT)
        P.dma("sp", cskv, cs_kv.rearrange("p (a b) -> p a b", a=NT), w=["cskv"])
        kst = [v3(ABF.alloc(4 * 512), a=4) for _ in range(2)]
        vst = [v4(ABF.alloc(4 * 4 * 130), a=4, b=4) for _ in range(2)]
        akst = [v3(ABF.alloc(2 * 512), a=2) for _ in range(2)]
        avst = [v4(ABF.alloc(4 * 4 * 66), a=4, b=4) for _ in range(2)]
        kbf = [ABF.alloc(768) for _ in range(2)]
        bkf = [AFL.alloc(512) for _ in range(2)]
        bvf = [AFL.alloc(512) for _ in range(2)]
        akvf = [AFL.alloc(512) for _ in range(2)]
        rta = AFL.alloc(512); rtb = AFL.alloc(512)
        for i in range(2):
            mset("pool", vst[i][:, :, :, 128:130], 1.0, [f"vst{i}"])
            mset("pool", avst[i][:, :, :, 64:66], 1.0, [f"avst{i}"])
        NG = NT // 4

        def mem_phase():
          for t in range(2):
              for c in range(8):
                  mm(banks[0][:, :], memb[:, c, t * 128:(t + 1) * 128], wm2[:, c, :], c == 0, c == 7, ["memb", "wm2"], [BK[0]])
              cp("dve", mkvf, banks[0][:, :], [], [BK[0], "mkvf"])
              P.dma("sp", mk_o[t * 128:(t + 1) * 128, :], mkvf[:, 0:256], r=["mkvf"], w=["o_mk"])
              P.dma("sp", mv_o[t * 128:(t + 1) * 128, :], mkvf[:, 256:512], r=["mkvf"], w=["o_mv"])
              cp("dve", mkbf, mkvf[:, 0:256], ["mkvf"], ["mkbf"])
              cp("dve", mvaug[:, t, :, 0:64], mkvf[:, 256:512].rearrange("p (h d) -> p h d", h=4), ["mkvf"], ["mvaug"])
              for j in range(2):
                  tr(banks_bf[1][:, j * 128:(j + 1) * 128], mkbf[:, j * 128:(j + 1) * 128], ident, ["mkbf", "ident"], [BK[1]])
              cp("dve", mkT[:, :, t * 128:(t + 1) * 128], banks_bf[1][:, 0:256].rearrange("p (a b) -> p a b", a=2), [], [BK[1], "mkT"])

        wq_loads = []
        wq_loads.append(lambda: P.dma("pool", wm2, w_mem.rearrange("(c p) n -> p c n", p=128), w=["wm2"]))
        wq_loads.append(lambda: P.dma("pool", memb, memT.rearrange("(c p) s -> p c s", p=128), w=["memb"]))
        n_mem_loads = len(wq_loads)
        for (lo, hi, slo, shi) in ((0, 256, 0, 256), (256, 512, 2304, 2560), (512, 1024, 768, 1280), (1024, 2048, 2560, 3584)):
            wq_loads.append(lambda lo=lo, hi=hi, slo=slo, shi=shi:
                            P.dma("pool", wq[:, :, lo:hi], w_in_v[:, :, slo:shi], w=["wq"] + MEMK))
        wq_loads.append(lambda: P.dma("pool", w_out_bf, w_out.rearrange("(c p) n -> p c n", p=128), w=["w_out_bf"]))
        mem_done = [False]

        def p1_mm(T):
            g, tl = T // 4, T % 4
            gb = g % 2
            if tl == 0 and g + 1 < NG:
                P.dma("pool", xta[(g + 1) % 2], xT_all_v[:, :, (g + 1) * 512:(g + 2) * 512], w=[f"xta{(g + 1) % 2}"])
                if g == 0:
                    for _ in range(n_mem_loads):
                        wq_loads.pop(0)()
                elif g >= 3:
                    if wq_loads:
                        wq_loads.pop(0)()
            pb = T % 2
            b0, b1, b2, b3 = (0, 1, 2, 3) if pb == 0 else (4, 5, 6, 7)
            for cb, bn in enumerate((b0, b1, b2)):
                for c in range(8):
                    mm(banks[bn][:, :], xta[gb][:, c, tl * 128:(tl + 1) * 128], wkv[:, c, cb * 512:(cb + 1) * 512],
                       c == 0, c == 7, [f"xta{gb}", "wkv%s" % ("2k" if cb == 2 else cb)], [BK[bn]])

        def p1_post(T):
            g, tl = T // 4, T % 4
            gb = g % 2
            pb = T % 2
            b0, b1, b2, b3 = (0, 1, 2, 3) if pb == 0 else (4, 5, 6, 7)
            cp("act", kbf[pb][:, 0:256], banks[b0][:, 0:256], [], [BK[b0], f"kbf{pb}"])
            cp("act", avst[gb][:, :, tl, 0:64], banks[b0][:, 256:512].rearrange("p (h d) -> p h d", h=4), [], [BK[b0], f"avst{gb}"])
            if T >= NT - 4:
                cp("dve", akvf[pb], banks[b0][:, :], [], [BK[b0], f"akvf{pb}"])
                r0 = (T - (NT - 4)) * 64
                P.dma("sp", ak_own[r0:r0 + 64, :], akvf[pb][0:64, 0:256], r=[f"akvf{pb}"], w=["o_ak"])
                P.dma("sp", av_own[r0:r0 + 64, :], akvf[pb][0:64, 256:512], r=[f"akvf{pb}"], w=["o_av"])
            rope(bkf[pb], banks[b1][:, :], cskv[:, T, :], 128, 8, ["cskv"], [BK[b1], f"bkf{pb}"], rta, rtb)
            P.dma("sp", bk_own[T * 64:(T + 1) * 64, :], bkf[pb][0:64, :], r=[f"bkf{pb}"], w=["o_bk"])
            cp("act", kbf[pb][:, 256:768], bkf[pb], [f"bkf{pb}"], [f"kbf{pb}"])
            cp("act", bvf[pb], banks[b2][:, :], [], [BK[b2], f"bvf{pb}"])
            P.dma("sp", bv_own[T * 64:(T + 1) * 64, :], bvf[pb][0:64, :], r=[f"bvf{pb}"], w=["o_bv"])
            cp("act", vst[gb][:, :, tl, 0:128], bvf[pb].rearrange("p (h d) -> p h d", h=4), [f"bvf{pb}"], [f"vst{gb}"])
            for j in range(6):
                tr(banks_bf[b3][:, j * 128:(j + 1) * 128], kbf[pb][:, j * 128:(j + 1) * 128], ident, [f"kbf{pb}", "ident"], [BK[b3]])
            cp("dve", akst[gb][:, :, tl * 128:(tl + 1) * 128], banks_bf[b3][:, 0:256].rearrange("p (a b) -> p a b", a=2), [], [BK[b3], f"akst{gb}"])
            cp("dve", kst[gb][:, :, tl * 128:(tl + 1) * 128], banks_bf[b3][:, 256:768].rearrange("p (a b) -> p a b", a=4), [], [BK[b3], f"kst{gb}"])
            if tl == 3:
                P.dma("sp", bkT_s.rearrange("h p s -> p h s")[:, :, g * 512:(g + 1) * 512], kst[gb], r=[f"kst{gb}"], w=["bkT_s"])
                P.dma("sp", akT_s.rearrange("h p s -> p h s")[:, :, g * 512:(g + 1) * 512], akst[gb], r=[f"akst{gb}"], w=["akT_s"])
                P.dma("sp", bv_s.rearrange("h p (t d) -> p h t d", d=130)[:, :, g * 4:(g + 1) * 4, :], vst[gb], r=[f"vst{gb}"], w=["bv_s"])
                P.dma("sp", av_s.rearrange("h p (t d) -> p h t d", d=66)[:, :, g * 4:(g + 1) * 4, :], avst[gb], r=[f"avst{gb}"], w=["av_s"])

        p1_mm(0)
        for T in range(NT):
            if T + 1 < NT:
                p1_mm(T + 1)
            p1_post(T)
            if T == 6:
                mem_phase()
        while wq_loads:
            wq_loads.pop(0)()

        P.barrier()
        ABF.off = bf_base; AFL.off = f_base
        P.op("pool", lambda e: e.memset(cvec[:, 8:9], 0.0), w=["cvec8"])
        abt = v4(AFL.alloc(4 * 5 * 64), a=4, b=5)
        P.dma("sp", abt, abias.rearrange("p (a b c) -> p a b c", a=4, b=5), w=["abt"])
        sabt = v4(AFL.alloc(4 * 5 * 16), a=4, b=5)
        P.dma("sp", sabt, sabias.rearrange("p (a b c) -> p a b c", a=4, b=5), w=["sabt"])
        css = AFL.alloc(64)
        P.dma("sp", css[0:16, :], cs_s[:, :], w=["css"])

        nt1 = AFL.alloc(128); nt2 = AFL.alloc(128)
        nsc = AFL.alloc(64)
        rta = AFL.alloc(512); rtb = AFL.alloc(512)
        gt4 = [AFL.alloc(1024) for _ in range(4)]
        xres4 = [AFL.alloc(1024) for _ in range(4)]
        zt4 = [AFL.alloc(1024) for _ in range(4)]
        xres, zt, yt, gtmp = xres4[0], zt4[0], gt4[1], gt4[0]
        qbf4 = [ABF.alloc(1024) for _ in range(4)]
        mgbf = qbf4[0]
        qT = ABF.alloc(4096)
        mgT4 = [v3(qT[:, u * 1024:(u + 1) * 1024], a=8) for u in range(4)]
        mgT = mgT4[0]
        bf_mark2 = ABF.off; f_mark2 = AFL.off
        aqT = v3(qT[:, 0:1024], a=2); mqT = v3(qT[:, 1024:2048], a=2); bqT = v3(qT[:, 2048:4096], a=4)
        xto = v3(ABF.alloc(8 * 512), a=8)
        NR = 3
        kseg = [ABF.alloc(1024) for _ in range(NR)]
        vseg = [v3(ABF.alloc(8 * 130), a=8) for _ in range(NR)]
        pT = [ABF.alloc(1024) for _ in range(3)]
        akw = v3(ABF.alloc(2 * 1536), a=2)
        avw = v4(ABF.alloc(4 * 12 * 66), a=4, b=12)
        apT = [ABF.alloc(320) for _ in range(8)]
        mpT = [ABF.alloc(512) for _ in range(4)]
        csqt = v3(AFL.alloc(4 * 64), a=4)
        mixed = v3(AFL.alloc(4 * 1024), a=4)

        def silu_gate(dst, src_psum, nparts, r, w):
            act(gtmp[0:nparts, :], src_psum, AF.Exp, 0.0, -1.0, r, w + ["gtmp"])
            ts("dve", gtmp[0:nparts, :], gtmp[0:nparts, :], 1.0, None, ALU.add, None, ["gtmp"], ["gtmp"])
            recip(gtmp[0:nparts, :], gtmp[0:nparts, :], ["gtmp"], ["gtmp"])
            tt("dve", dst, gtmp[0:nparts, :], src_psum, ALU.mult, ["gtmp"] + r, w + ["sgw"])

        def norm_plain(dst, o_ps, nparts, dv, r, w, p0=0, col=7):
            recip(nsc[p0:p0 + nparts, col:col + 1], o_ps[:, dv:dv + 1], r, w + [f"nsc{col}"])
            ts("dve", dst, o_ps[:, 0:dv], nsc[p0:p0 + nparts, col:col + 1], None, ALU.mult, None, [f"nsc{col}"] + r, w)

        def norm_diff(dst, o1, o2, nparts, r, w):
            recip(nsc[0:nparts, 0:1], o1[:, 128:129], r, w + ["nsc"])
            recip(nsc[0:nparts, 1:2], o2[:, 128:129], r, w + ["nsc1"])
            tt("dve", nsc[0:nparts, 2:3], nsc[0:nparts, 1:2], cvec[0:nparts, 1:2], ALU.mult, ["nsc1", "cvec"], ["nsc2"])
            ts("dve", nt1[0:nparts, :], o1[:, 0:128], nsc[0:nparts, 0:1], None, ALU.mult, None, ["nsc"] + r, w + ["nt1"])
            stt("dve", nt2[0:nparts, :], o2[:, 0:128], nsc[0:nparts, 2:3], nt1[0:nparts, :], ALU.mult, ALU.add, ["nsc2", "nt1"] + r, w + ["nt2"])
            tt("dve", nt1[0:nparts, :], nt2[0:nparts, :], nt2[0:nparts, :], ALU.mult, ["nt2"], ["nt1"])
            rsum(nsc[0:nparts, 3:4], nt1[0:nparts, :], ["nt1"], ["nsc3"])
            ts("dve", nsc[0:nparts, 4:5], nsc[0:nparts, 3:4], 1.0 / 128, 1e-5, ALU.mult, ALU.add, ["nsc3"], ["nsc4"])
            act(nsc[0:nparts, 5:6], nsc[0:nparts, 4:5], AF.Ln, 0.0, 1.0, ["nsc4"], ["nsc5"])
            act(nsc[0:nparts, 6:7], nsc[0:nparts, 5:6], AF.Exp, 0.0, -0.5, ["nsc5"], ["nsc6"])
            stt("dve", dst, nt2[0:nparts, :], nsc[0:nparts, 6:7], gsubt[0:nparts, :], ALU.mult, ALU.mult, ["nsc6", "nt2", "gsubt"], w)

        def out_phase(mix, sgt, xr_src, dst_dram, nparts, bA, bB, bT):
            P.dma("sp", xres[0:nparts, :], xr_src, w=["xres"])
            tt("dve", mgbf[0:nparts, :], mix, sgt, ALU.mult, ["mixed", "mixB", "sgw"], ["mgbf"])
            for j in range(8):
                tr(banks_bf[bT][:, j * 128:j * 128 + nparts], mgbf[0:nparts, j * 128:(j + 1) * 128], ident[0:nparts, 0:nparts],
                   ["mgbf", "ident"], [BK[bT]])
            cp("dve", mgT[:, :, 0:nparts], banks_bf[bT][:, 0:1024].rearrange("p (a b) -> p a b", a=8)[:, :, 0:nparts], [], [BK[bT], "mgT"])
            for cb, bn in enumerate((bA, bB)):
                for e_ in range(8):
                    mm(banks[bn][0:nparts, :], mgT[:, e_, 0:nparts], w_out_bf[:, e_, cb * 512:(cb + 1) * 512], e_ == 0, e_ == 7,
                       ["mgT", "w_out_bf"], [BK[bn]])
            for cb, bn in enumerate((bA, bB)):
                stt("dve", zt[0:nparts, cb * 512:(cb + 1) * 512], xres[0:nparts, cb * 512:(cb + 1) * 512], ALPHA, banks[bn][0:nparts, :],
                    ALU.mult, ALU.add, ["xres"], [BK[bn], "zt0", "zt0b"])
            rsum(nsc[0:nparts, 8:9], zt[0:nparts, :], ["zt0", "zt0b"], ["ln0"])
            ts("dve", nsc[0:nparts, 9:10], nsc[0:nparts, 8:9], -1.0 / 1024, None, ALU.mult, None, ["ln0"], ["ln1"])
            ts("dve", zt[0:nparts, :], zt[0:nparts, :], nsc[0:nparts, 9:10], None, ALU.add, None, ["ln1", "zt0", "zt0b"], ["zt0", "zt0b"])
            tt("dve", yt[0:nparts, :], zt[0:nparts, :], zt[0:nparts, :], ALU.mult, ["zt0", "zt0b"], ["yt"])
            rsum(nsc[0:nparts, 10:11], yt[0:nparts, :], ["yt"], ["ln2"])
            ts("dve", nsc[0:nparts, 11:12], nsc[0:nparts, 10:11], 1.0 / 1024, 1e-5, ALU.mult, ALU.add, ["ln2"], ["ln3"])
            act(nsc[0:nparts, 12:13], nsc[0:nparts, 11:12], AF.Ln, 0.0, 1.0, ["ln3"], ["ln4"])
            act(nsc[0:nparts, 13:14], nsc[0:nparts, 12:13], AF.Exp, 0.0, -0.5, ["ln4"], ["ln5"])
            stt("dve", yt[0:nparts, :], zt[0:nparts, :], nsc[0:nparts, 13:14], lngt[0:nparts, :], ALU.mult, ALU.mult, ["ln5", "zt0", "zt0b", "lngt"], ["yt"])
            tt("dve", yt[0:nparts, :], yt[0:nparts, :], lnbt[0:nparts, :], ALU.add, ["yt", "lnbt"], ["yt"])
            P.dma("sp", dst_dram, yt[0:nparts, :], r=["yt"], w=["o_y"])

        bkT_v = bkT_s
        bv_v = bv_s.rearrange("h p (t d) -> h p t d", d=130)
        akT_v = akT_s.rearrange("h p s -> p h s")
        av_v = av_s.rearrange("h p (t d) -> p h t d", d=66)
        xT_own_v = xT_own.rearrange("(c p) s -> p c s", p=128)
        cs_q_v = cs_q.rearrange("p (a b) -> p a b", b=64)
        OREG = [(4, 0), (4, 256), (5, 0), (5, 256), (6, 0), (6, 256), (7, 0), (7, 256)]

        def q_load(sp):
            P.dma("pool", xto, xT_own_v[:, :, sp * 512:(sp + 1) * 512], w=["xto"])
            P.dma("sp", csqt, cs_q_v[:, sp * 4:(sp + 1) * 4, :], w=["csqt"])

        def q_phase(sp):
            for u in range(4):
                for cb in range(2):
                    bn = 2 * u + cb
                    for c_ in range(8):
                        mm(banks[bn][:, :], xto[:, c_, u * 128:(u + 1) * 128], wq[:, c_, cb * 512:(cb + 1) * 512], c_ == 0, c_ == 7,
                           ["xto", "wq"], [BK[bn]])
            for u in range(4):
                cp("act", qbf4[u][:, 0:512], banks[2 * u][:, :], [], [BK[2 * u], f"qbf{u}"])
                rope(qbf4[u][:, 512:1024], banks[2 * u + 1][:, :], csqt[:, u, :], 128, 8, ["csqt"], [BK[2 * u + 1], f"qbf{u}"], rta, rtb)
            for u in range(4):
                for j in range(8):
                    tr(banks_bf[2 * u][:, j * 128:(j + 1) * 128], qbf4[u][:, j * 128:(j + 1) * 128], ident, [f"qbf{u}", "ident"], [BK[2 * u]])
            for u in range(4):
                us = slice(u * 128, (u + 1) * 128)
                cp("act", aqT[:, :, us], banks_bf[2 * u][:, 0:256].rearrange("p (a b) -> p a b", a=2), [], [BK[2 * u], f"aqT{u}"])
                cp("act", mqT[:, :, us], banks_bf[2 * u][:, 256:512].rearrange("p (a b) -> p a b", a=2), [], [BK[2 * u], f"mqT{u}"])
                cp("dve", bqT[:, :, us], banks_bf[2 * u][:, 512:1024].rearrange("p (a b) -> p a b", a=4), [], [BK[2 * u], f"bqT{u}"])
        QK_R = [f"aqT{u}" for u in range(4)]
        MQ_R = [f"mqT{u}" for u in range(4)]
        BQ_R = [f"bqT{u}" for u in range(4)]
        ALIAS = {0: QK_R, 1: MQ_R, 2: BQ_R, 3: BQ_R}

        def a_phase(sp):
            t_lo = max(0, sp * 8 - 4)
            nload = sp * 8 + 8 - t_lo
            off_t = t_lo - (sp * 8 - 4)
            P.dma("sp", akw[:, :, off_t * 128:(off_t + nload) * 128], akT_v[:, :, t_lo * 128:(t_lo + nload) * 128], r=["akT_s"], w=["akw"])
            P.dma("sp", avw[:, :, off_t:off_t + nload, :], av_v[:, :, t_lo:t_lo + nload, :], r=["av_s"], w=["avw"])

            def jbs_of(c_):
                C = sp * 8 + c_
                return [jb for jb in range(5) if C - 4 + jb >= 0]

            def qk(g):
                jbs = jbs_of(g)
                for hd in range(4):
                    bn = (g % 2) * 4 + hd
                    pr, hp = hd // 2, (hd % 2) * 64
                    for jb in jbs:
                        wt = g + jb
                        mm(banks[bn][:, jb * 64:(jb + 1) * 64], akw[hp:hp + 64, pr, wt * 128:(wt + 1) * 128],
                           aqT[hp:hp + 64, pr, g * 64:(g + 1) * 64], True, True, ["akw"] + QK_R, [BK[bn]])

            def pre(g):
                j0 = jbs_of(g)[0]
                for hd in range(4):
                    bn = (g % 2) * 4 + hd
                    stt("dve", banks[bn][:, j0 * 64:320], banks[bn][:, j0 * 64:320], 0.125,
                        abt[:, hd, j0:5, :].rearrange("p a b -> p (a b)"), ALU.mult, ALU.add, ["abt"], [BK[bn]])

            def ex(g):
                j0 = jbs_of(g)[0]
                for hd in range(4):
                    bn = (g % 2) * 4 + hd
                    act(apT[bn][:, j0 * 64:320], banks[bn][:, j0 * 64:320], AF.Exp, 0.0, 1.0, [], [BK[bn], f"apT{bn}"])

            def pv(g):
                jbs = jbs_of(g)
                p0 = 64 * (g % 2)
                for hd in range(4):
                    bn = (g % 2) * 4 + hd
                    ob = (g % 2) * 4 + (hd // 2) * 2
                    oc = 320 + (hd % 2) * 80
                    for jb in jbs:
                        wt = g + jb
                        mm(banks[ob][p0:p0 + 64, oc:oc + 65], apT[bn][:, jb * 64:(jb + 1) * 64], avw[:, hd, wt, 0:65],
                           jb == jbs[0] and hd % 2 == 0, jb == jbs[-1], [f"apT{bn}", "avw"], [BK[ob]], skip=True)

            def nrm(g):
                p0 = 64 * (g % 2)
                for hp_ in range(2):
                    ob = (g % 2) * 4 + hp_ * 2
                    o3 = banks[ob][p0:p0 + 64, 320:480].rearrange("p (h c) -> p h c", h=2)
                    scn = nsc[p0:p0 + 64, 16 + 2 * hp_:18 + 2 * hp_]
                    recip(scn, o3[:, :, 64], [], [BK[ob], f"nscA{hp_}"])
                    tt("dve", mixed[p0:p0 + 64, g // 2, hp_ * 128:(hp_ + 1) * 128].rearrange("p (h c) -> p h c", h=2), o3[:, :, 0:64],
                       scn.unsqueeze(2).to_broadcast([64, 2, 64]), ALU.mult, [f"nscA{hp_}"], [BK[ob], "mixA"])
            qk(0); pre(0); ex(0)
            for g in range(8):
                if g + 1 < 8:
                    qk(g + 1); pre(g + 1)
                pv(g)
                if g + 1 < 8:
                    ex(g + 1)
                nrm(g)

        def m_phase(sp):
            MREG = [0, 80, 160, 240]

            def qk(hh):
                for hd in (hh, hh + 1):
                    pr, hp = hd // 2, (hd % 2) * 64
                    for kt in range(2):
                        bn = (hh // 2) * 4 + (hd % 2) * 2 + kt
                        mm(banks[bn][:, :], mkT[hp:hp + 64, pr, kt * 128:(kt + 1) * 128], mqT[hp:hp + 64, pr, 0:512], True, True,
                           ["mkT"] + MQ_R, [BK[bn]])

            def ex(hh):
                for hd in (hh, hh + 1):
                    for kt in range(2):
                        bn = (hh // 2) * 4 + (hd % 2) * 2 + kt
                        act(mpT[(hd % 2) * 2 + kt][:, :], banks[bn][:, :], AF.Exp, 0.0, 0.125, [], [BK[bn], f"mpT{(hd % 2) * 2 + kt}"])

            def pv(hh):
                for hd in (hh, hh + 1):
                    bn = (hh // 2) * 4 + (hd % 2) * 2
                    for kt in range(2):
                        for qb in range(4):
                            col = MREG[qb]
                            mm(banks[bn][:, col:col + 65], mpT[(hd % 2) * 2 + kt][:, qb * 128:(qb + 1) * 128], mvaug[:, kt, hd, 0:65],
                               kt == 0 and qb == 0, kt == 1, [f"mpT{(hd % 2) * 2 + kt}", "mvaug"], [BK[bn]], skip=True)

            def nrm(hh):
                for hd in (hh, hh + 1):
                    bn = (hh // 2) * 4 + (hd % 2) * 2
                    o3 = banks[bn][:, 0:320].rearrange("p (q c) -> p q c", q=4)
                    scn = nsc[:, 20 + (hd % 2) * 4:24 + (hd % 2) * 4]
                    recip(scn, o3[:, :, 64], [], [BK[bn], f"nscM{hd % 2}"])
                    tt("dve", mixed[:, :, 768 + hd * 64:768 + (hd + 1) * 64], o3[:, :, 0:64], scn.unsqueeze(2).to_broadcast([128, 4, 64]),
                       ALU.mult, [f"nscM{hd % 2}"], [BK[bn], "mixM"])
            qk(0); ex(0); qk(2); pv(0); ex(2); nrm(0); pv(2); nrm(2)

        def out_phase_sp(sp):
            def tiles(g):
                return [(2 * g + t, 4 * g + 2 * t, 4 * g + 2 * t + 1) for t in range(2)]

            def s1(g):
                for u, b0, b1 in tiles(g):
                    U = sp * 4 + u
                    P.dma("sp", xres4[u], x_own[U * 128:(U + 1) * 128, :], w=[f"xres{u}"])
                    for cb, bn in ((0, b0), (1, b1)):
                        for c_ in range(8):
                            mm(banks[bn][:, :], xto[:, c_, u * 128:(u + 1) * 128], wq[:, c_, 1024 + cb * 512:1024 + (cb + 1) * 512],
                               c_ == 0, c_ == 7, ["xto", "wq"], [BK[bn]])

            def s2(g):
                for u, b0, b1 in tiles(g):
                    for cb, bn in ((0, b0), (1, b1)):
                        act(gt4[u][:, cb * 512:(cb + 1) * 512], banks[bn][:, :], AF.Exp, 0.0, -1.0, [], [BK[bn], f"gt{u}"])

            def s3(g):
                for u, b0, b1 in tiles(g):
                    act(gt4[u], gt4[u], AF.Ln, 1.0, 1.0, [f"gt{u}"], [f"gt{u}"])
                for u, b0, b1 in tiles(g):
                    act(gt4[u], gt4[u], AF.Exp, 0.0, -1.0, [f"gt{u}"], [f"gt{u}"])

            def s4(g):
                for u, b0, b1 in tiles(g):
                    for cb, bn in ((0, b0), (1, b1)):
                        tt("dve", gt4[u][:, cb * 512:(cb + 1) * 512], gt4[u][:, cb * 512:(cb + 1) * 512], banks[bn][:, :], ALU.mult,
                           [f"gt{u}"], [BK[bn], f"gt{u}"])
                    tt("dve", qbf4[u], mixed[:, u, :], gt4[u], ALU.mult, ["mixA", "mixB", "mixM", f"gt{u}"], [f"qbf{u}"])

            def s5(g):
                for u, b0, b1 in tiles(g):
                    for j in range(8):
                        tr(banks_bf[b0][:, j * 128:(j + 1) * 128], qbf4[u][:, j * 128:(j + 1) * 128], ident, [f"qbf{u}", "ident"], [BK[b0]])

            def s6(g):
                for u, b0, b1 in tiles(g):
                    cp("act", mgT4[u], banks_bf[b0][:, 0:1024].rearrange("p (a b) -> p a b", a=8), [], [BK[b0], f"mgT{u}"] + ALIAS[u])

            def s7(g):
                for u, b0, b1 in tiles(g):
                    for cb, bn in ((0, b0), (1, b1)):
                        for e_ in range(8):
                            mm(banks[bn][:, :], mgT4[u][:, e_, :], w_out_bf[:, e_, cb * 512:(cb + 1) * 512], e_ == 0, e_ == 7,
                               [f"mgT{u}", "w_out_bf"] + ALIAS[u], [BK[bn]])

            def s8(g):
                for u, b0, b1 in tiles(g):
                    for cb, bn in ((0, b0), (1, b1)):
                        stt("dve", zt4[u][:, cb * 512:(cb + 1) * 512], xres4[u][:, cb * 512:(cb + 1) * 512], ALPHA, banks[bn][:, :],
                            ALU.mult, ALU.add, [f"xres{u}"], [BK[bn], f"zt{u}"])
                    rsum(nsc[:, 32 + u:33 + u], zt4[u], [f"zt{u}"], [f"lna{u}"])

            def s9(g):
                for u, b0, b1 in tiles(g):
                    ts("dve", nsc[:, 32 + u:33 + u], nsc[:, 32 + u:33 + u], -1.0 / 1024, None, ALU.mult, None, [f"lna{u}"], [f"lna{u}"])
                    act(gt4[u], zt4[u], AF.Square, nsc[:, 32 + u:33 + u], 1.0, [f"lna{u}", f"zt{u}"], [f"gt{u}"])

            def s10(g):
                for u, b0, b1 in tiles(g):
                    rsum(nsc[:, 40 + u:41 + u], gt4[u], [f"gt{u}"], [f"lnb{u}"])
                    ts("dve", nsc[:, 40 + u:41 + u], nsc[:, 40 + u:41 + u], 1.0 / 1024, 1e-5, ALU.mult, ALU.add, [f"lnb{u}"], [f"lnb{u}"])

            def s11(g):
                for u, b0, b1 in tiles(g):
                    act(nsc[:, 40 + u:41 + u], nsc[:, 40 + u:41 + u], AF.Ln, 0.0, 1.0, [f"lnb{u}"], [f"lnb{u}"])
                    act(nsc[:, 40 + u:41 + u], nsc[:, 40 + u:41 + u], AF.Exp, 0.0, -0.5, [f"lnb{u}"], [f"lnb{u}"])

            def s12(g):
                for u, b0, b1 in tiles(g):
                    tt("dve", nsc[:, 32 + u:33 + u], nsc[:, 32 + u:33 + u], nsc[:, 40 + u:41 + u], ALU.mult, [f"lna{u}", f"lnb{u}"], [f"lna{u}"])
                    act(gt4[u], zt4[u], AF.Identity, nsc[:, 32 + u:33 + u], nsc[:, 40 + u:41 + u], [f"lna{u}", f"lnb{u}", f"zt{u}"], [f"gt{u}"])

            def s13(g):
                for u, b0, b1 in tiles(g):
                    U = sp * 4 + u
                    tt("pool", gt4[u], gt4[u], lngt, ALU.mult, [f"gt{u}", "lngt"], [f"gt{u}"])
                    tt("pool", zt4[u], gt4[u], lnbt, ALU.add, [f"gt{u}", "lnbt"], [f"zt{u}"])
                    P.dma("sp", y_own[U * 128:(U + 1) * 128, :], zt4[u], r=[f"zt{u}"], w=["o_y"])
            for stg in (s1, s2, s3, s4, s5, s6, s7, s8, s9, s10, s11, s12, s13):
                stg(0); stg(1)
                if stg is s1 and sp + 1 < nspan:
                    q_load(sp + 1)

        def norm_diff4(o1, o2, k1, k2, dst3, np_, split=False):
            t1 = zt4[0][0:np_, 0:512].rearrange("p (q c) -> p q c", q=4)
            t2 = zt4[0][0:np_, 512:1024].rearrange("p (q c) -> p q c", q=4)
            sq = zt4[1][0:np_, 0:512].rearrange("p (q c) -> p q c", q=4)
            sc = nsc[0:np_, 48:64]

            def bc(ap2):
                return ap2.unsqueeze(2).to_broadcast([np_, 4, 128])
            recip(sc[:, 0:4], o1[:, :, 128], [], k1 + ["nd_r1"])
            recip(sc[:, 4:8], o2[:, :, 128], [], k2 + ["nd_r2"])
            ts("dve", sc[:, 4:8], sc[:, 4:8], cvec[0:np_, 1:2], None, ALU.mult, None, ["nd_r2", "cvec"], ["nd_r2"])
            tt("dve", t1, o1[:, :, 0:128], bc(sc[:, 0:4]), ALU.mult, ["nd_r1"], k1 + ["zt0"])
            tt("dve", t2, o2[:, :, 0:128], bc(sc[:, 4:8]), ALU.mult, ["nd_r2"], k2 + ["zt0b"])
            tt("dve", t1, t1, t2, ALU.add, ["zt0", "zt0b"], ["zt0"])
            tt("dve", sq, t1, t1, ALU.mult, ["zt0"], ["zt1"])
            rsum(sc[:, 8:12], sq, ["zt1"], ["nd_ss"])
            ts("dve", sc[:, 8:12], sc[:, 8:12], 1.0 / 128, 1e-5, ALU.mult, ALU.add, ["nd_ss"], ["nd_ss"])

            def part2():
                act(sc[:, 12:16], sc[:, 8:12], AF.Ln, 0.0, 1.0, ["nd_ss"], ["nd_rs"])
                act(sc[:, 12:16], sc[:, 12:16], AF.Exp, 0.0, -0.5, ["nd_rs"], ["nd_rs"])
                tt("dve", t1, t1, bc(sc[:, 12:16]), ALU.mult, ["nd_rs", "zt0"], ["zt0"])
                tt("dve", dst3, t1, gsubt[0:np_, :].unsqueeze(1).to_broadcast([np_, 4, 128]), ALU.mult, ["zt0", "gsubt"], ["mixB"])
            if split:
                return part2
            part2()
            return None

        segctr = [0]

        b_slots = {}

        def b_load_seg(sp, j):
            segs = [(hd, s_) for hd in range(4) for s_ in range(sp + 1)]
            slot_of = b_slots.setdefault(sp, {})
            if j >= len(segs) or j in slot_of:
                return
            hd, s_ = segs[j]
            slot = segctr[0] % NR; segctr[0] += 1
            slot_of[j] = slot
            P.dma("sp", kseg[slot], bkT_v[hd, :, s_ * 1024:(s_ + 1) * 1024], r=["bkT_s"], w=[f"kseg{slot}"])
            P.dma("sp", vseg[slot][:, :, :], bv_v[hd, :, s_ * 8:(s_ + 1) * 8, :], r=["bv_s"], w=[f"vseg{slot}"])

        def b_phase(sp):
            slot_of = b_slots.setdefault(sp, {})

            def load_seg(j):
                b_load_seg(sp, j)

            pend = [None]
            for hd in range(4):
                tiles = [(s_, kt) for s_ in range(sp + 1) for kt in range(8)]
                nk = len(tiles)

                def front_qk(n):
                    s_, kt = tiles[n]
                    j = hd * (sp + 1) + s_
                    if kt == 0:
                        load_seg(j)
                        load_seg(j + 1)
                    slot = slot_of[j]
                    diag = (s_ == sp)
                    c0 = 64 * kt if diag else 0
                    pb_ = n % 2
                    for m in range(2):
                        bn = 2 * pb_ + m
                        mm(banks[bn][:, c0:512], kseg[slot][m * 64:(m + 1) * 64, kt * 128:(kt + 1) * 128],
                           bqT[m * 64:(m + 1) * 64, hd, c0:512], True, True, [f"kseg{slot}"] + BQ_R, [BK[bn]])

                def front_exp(n):
                    s_, kt = tiles[n]
                    diag = (s_ == sp)
                    c0 = 64 * kt if diag else 0
                    pb_ = n % 2
                    tb_ = n % 3
                    bk2 = [BK[2 * pb_], BK[2 * pb_ + 1]]
                    src3 = pairs[pb_][:, :].rearrange("p (m c) -> p m c", m=2)
                    dst3 = pT[tb_].rearrange("p (m c) -> p m c", m=2)
                    if diag:
                        act(dst3[:, :, c0:c0 + 64], src3[:, :, c0:c0 + 64], AF.Exp, cvec[:, 0:1], 0.125, ["cvec"], bk2 + [f"pT{tb_}"])
                        if c0 + 64 < 512:
                            act(dst3[:, :, c0 + 64:512], src3[:, :, c0 + 64:512], AF.Exp, 0.0, 0.125, [], bk2 + [f"pT{tb_}"])
                    else:
                        act(pT[tb_], pairs[pb_][:, :], AF.Exp, 0.0, 0.125, [], bk2 + [f"pT{tb_}"])

                def back(n):
                    s_, kt = tiles[n]
                    slot = slot_of[hd * (sp + 1) + s_]
                    diag = (s_ == sp)
                    tb_ = n % 3
                    qb0 = (kt // 2) if diag else 0
                    first_bank = set()
                    for m in range(2):
                        for qb in range(qb0, 4):
                            bn, col = OREG[m * 4 + qb]
                            st_ = (n == 0 and bn not in first_bank)
                            first_bank.add(bn)
                            last = (diag and kt == 2 * qb + 1)
                            if diag and kt % 2 == 1 and qb == qb0:
                                mm(banks[bn][64:128, col:col + 129], pT[tb_][:, m * 512 + qb * 128 + 64:m * 512 + (qb + 1) * 128],
                                   vseg[slot][:, kt, 0:129], st_, last, [f"pT{tb_}", f"vseg{slot}"], [BK[bn]], skip=True)
                            else:
                                mm(banks[bn][:, col:col + 129], pT[tb_][:, m * 512 + qb * 128:m * 512 + (qb + 1) * 128], vseg[slot][:, kt, 0:129],
                                   st_, last, [f"pT{tb_}", f"vseg{slot}"], [BK[bn]], skip=True)
                front_qk(0)
                for n in range(nk):
                    front_exp(n)
                    if n + 1 < nk:
                        front_qk(n + 1)
                    if n == 2 and pend[0] is not None:
                        pend[0](); pend[0] = None
                    if n > 0:
                        back(n - 1)
                back(nk - 1)
                if pend[0] is not None:
                    pend[0](); pend[0] = None
                pend[0] = norm_diff4(pairs[2][:, :].rearrange("p (q c) -> p q c", q=4), pairs[3][:, :].rearrange("p (q c) -> p q c", q=4),
                                     [BK[4], BK[5]], [BK[6], BK[7]], mixed[:, :, 256 + hd * 128:256 + (hd + 1) * 128], 128, split=True)
            if pend[0] is not None:
                pend[0](); pend[0] = None

        q_load(0)
        for sp_ in range(nspan):
            b_load_seg(sp_, 0); b_load_seg(sp_, 1)
            q_phase(sp_)
            b_phase(sp_)
            a_phase(sp_)
            m_phase(sp_)
            out_phase_sp(sp_)

        P.barrier()
        ABF.off = bf_mark2; AFL.off = f_mark2
        wkv2 = v3(ABF.alloc(8 * 1536), a=8)
        P.dma("sp", wkv2, wkv_s.rearrange("p (c n) -> p c n", c=8), w=["wkv2"])
        xsb = v3(ABF.alloc(8 * 64), a=8)
        P.dma("pool", xsb, xsT.rearrange("(c p) s -> p c s", p=128), w=["xsb"])
        qsb = ABF.alloc(1792)
        sT = v3(ABF.alloc(14 * 16), a=14)
        sbva = v3(ABF.alloc(4 * 130), a=4); sava = v3(ABF.alloc(4 * 66), a=4)
        ckb = v3(ABF.alloc(4 * 1024), a=4); cvb_f = ABF.alloc(8 * 4 * 130); cvb = v4(cvb_f, a=8, b=4)
        cka = v3(ABF.alloc(2 * 512), a=2); cva_f = ABF.alloc(4 * 4 * 66); cva = v4(cva_f, a=4, b=4)
        ckm = v3(ABF.alloc(2 * 256), a=2); cvm_f = ABF.alloc(2 * 4 * 66); cvm = v4(cvm_f, a=2, b=4)
        spTb = v4(qT[:, 1024:2176], a=4, b=2)
        spTa = v4(qT[:, 2176:2496], a=2, b=2)
        spTm = v3(qT[:, 2496:2624], a=2)
        sgs = AFL.alloc(1024); mixs = AFL.alloc(1024)
        skvf = AFL.alloc(512); sbkf = AFL.alloc(512); sbvf = AFL.alloc(512); sbqf = AFL.alloc(512)
        mset("pool", sbva[:, :, 128:130], 1.0, ["sbva"]); mset("pool", sava[:, :, 64:66], 1.0, ["sava"])
        mset("pool", cvb[:, :, :, 128:130], 1.0, ["cvb"]); mset("pool", cva[:, :, :, 64:66], 1.0, ["cva"])
        mset("pool", cvm[:, :, :, 64:66], 1.0, ["cvm"])
        for bb in range(4):
            P.dma("pool", ckb, cbkT[bb].rearrange("h p s -> p h s"), w=["ckb"])
            P.dma("pool", cvb_f.rearrange("p (g d) -> p g d", d=130)[:, :, 0:128], cbv[bb].rearrange("p (g d) -> p g d", d=128), w=["cvb"])
            P.dma("pool", cka, cakT[bb].rearrange("h p s -> p h s"), w=["cka"])
            P.dma("pool", cva_f.rearrange("p (g d) -> p g d", d=66)[:, :, 0:64], cav[bb].rearrange("p (g d) -> p g d", d=64), w=["cva"])
            P.dma("pool", ckm, cmkT[bb].rearrange("h p s -> p h s"), w=["ckm"])
            P.dma("pool", cvm_f.rearrange("p (g d) -> p g d", d=66)[:, :, 0:64], cmv[bb].rearrange("p (g d) -> p g d", d=64), w=["cvm"])
            tok = slice(bb * 16, (bb + 1) * 16)
            for cb in range(7):
                wsrc = wq if cb < 4 else wkv2
                cbo = cb if cb < 4 else cb - 4
                for c in range(8):
                    mm(banks[cb][0:16, :], xsb[:, c, tok], wsrc[:, c, cbo * 512:(cbo + 1) * 512], c == 0, c == 7,
                       ["xsb", "wq", "wkv2"], [BK[cb]])
            cp("act", qsb[0:16, 0:512], banks[0][0:16, :], [], [BK[0], "qsb"])
            rope(sbqf[0:16, :], banks[1][0:16, :], css[0:16, :], 16, 8, ["css"], [BK[1], "sbqf"], rta[0:16, :], rtb[0:16, :])
            cp("act", qsb[0:16, 512:1024], sbqf[0:16, :], ["sbqf"], ["qsb"])
            for half, bn in enumerate((2, 3)):
                hs = slice(half * 512, (half + 1) * 512)
                act(gtmp[0:16, hs], banks[bn][0:16, :], AF.Exp, 0.0, -1.0, [], [BK[bn], "gtmp"])
                act(gtmp[0:16, hs], gtmp[0:16, hs], AF.Ln, 1.0, 1.0, ["gtmp"], ["gtmp"])
                act(gtmp[0:16, hs], gtmp[0:16, hs], AF.Exp, 0.0, -1.0, ["gtmp"], ["gtmp"])
                tt("dve", sgs[0:16, hs], gtmp[0:16, hs], banks[bn][0:16, :], ALU.mult, ["gtmp"], [BK[bn], "sgw"])
            cp("dve", skvf[0:16, :], banks[4][0:16, :], [], [BK[4], "skvf"])
            P.dma("sp", sak[tok, :], skvf[0:16, 0:256], r=["skvf"], w=["o_sak"])
            P.dma("sp", sav[tok, :], skvf[0:16, 256:512], r=["skvf"], w=["o_sav"])
            cp("act", qsb[0:16, 1024:1280], skvf[0:16, 0:256], ["skvf"], ["qsb"])
            cp("dve", sava[0:16, :, 0:64], skvf[0:16, 256:512].rearrange("p (h d) -> p h d", h=4), ["skvf"], ["sava"])
            rope(sbkf[0:16, :], banks[5][0:16, :], css[0:16, :], 16, 8, ["css"], [BK[5], "sbkf"], rta[0:16, :], rtb[0:16, :])
            P.dma("sp", sbk[tok, :], sbkf[0:16, :], r=["sbkf"], w=["o_sbk"])
            cp("act", qsb[0:16, 1280:1792], sbkf[0:16, :], ["sbkf"], ["qsb"])
            cp("dve", sbvf[0:16, :], banks[6][0:16, :], [], [BK[6], "sbvf"])
            P.dma("sp", sbv[tok, :], sbvf[0:16, :], r=["sbvf"], w=["o_sbv"])
            cp("dve", sbva[0:16, :, 0:128], sbvf[0:16, :].rearrange("p (h d) -> p h d", h=4), ["sbvf"], ["sbva"])
            for j in range(14):
                tr(banks_bf[7][:, j * 16:(j + 1) * 16], qsb[0:16, j * 128:(j + 1) * 128], ident[0:16, 0:16], ["qsb", "ident"], [BK[7]])
            cp("dve", sT, banks_bf[7][:, 0:224].rearrange("p (a b) -> p a b", a=14), [], [BK[7], "sT"])
            for m in range(2):
                ms = slice(m * 64, (m + 1) * 64)
                for hd in range(4):
                    bn = 2 * m + hd // 2; c0 = (hd % 2) * 144
                    for kt in range(8):
                        mm(banks[bn][:, c0 + kt * 16:c0 + (kt + 1) * 16], ckb[ms, hd, kt * 128:(kt + 1) * 128], sT[ms, 4 + hd, :], True, True,
                           ["ckb", "sT"], [BK[bn]])
                    mm(banks[bn][0:16, c0 + 128:c0 + 144], sT[ms, 10 + hd, :], sT[ms, 4 + hd, :], True, True, ["sT"], [BK[bn]])
            for hd in range(4):
                pr, hp = hd // 2, (hd % 2) * 64
                hsl = slice(hp, hp + 64)
                bn = 4 + hd % 2; c0 = (hd // 2) * 80
                for kt in range(4):
                    mm(banks[bn][:, c0 + kt * 16:c0 + (kt + 1) * 16], cka[hsl, pr, kt * 128:(kt + 1) * 128], sT[hsl, 0 + pr, :], True, True,
                       ["cka", "sT"], [BK[bn]])
                mm(banks[bn][0:16, c0 + 64:c0 + 80], sT[hsl, 8 + pr, :], sT[hsl, 0 + pr, :], True, True, ["sT"], [BK[bn]])
                bn = 6 + hd % 2; c0 = (hd // 2) * 32
                for kt in range(2):
                    mm(banks[bn][:, c0 + kt * 16:c0 + (kt + 1) * 16], ckm[hsl, pr, kt * 128:(kt + 1) * 128], sT[hsl, 2 + pr, :], True, True,
                       ["ckm", "sT"], [BK[bn]])
            for bn in range(4):
                src3 = banks[bn][:, 0:288].rearrange("p (h c) -> p h c", h=2)
                act(spTb[:, bn, :, 0:128], src3[:, :, 0:128], AF.Exp, 0.0, 0.125, [], [BK[bn], "spTb"])
                act(spTb[0:16, bn, :, 128:144], src3[0:16, :, 128:144], AF.Exp, 0.0, 0.125, [], [BK[bn], "spTb"])
            for hd in range(4):
                bn = 4 + hd % 2; c0 = (hd // 2) * 80
                stt("dve", banks[bn][:, c0:c0 + 64], banks[bn][:, c0:c0 + 64], 0.125, sabt[:, hd, 0:4, :].rearrange("p a b -> p (a b)"),
                    ALU.mult, ALU.add, ["sabt"], [BK[bn]])
                stt("dve", banks[bn][0:16, c0 + 64:c0 + 80], banks[bn][0:16, c0 + 64:c0 + 80], 0.125, sabt[0:16, hd, 4, :],
                    ALU.mult, ALU.add, ["sabt"], [BK[bn]])
            for p_ in range(2):
                src3 = banks[4 + p_][:, 0:160].rearrange("p (h c) -> p h c", h=2)
                act(spTa[:, p_, :, 0:64], src3[:, :, 0:64], AF.Exp, 0.0, 1.0, [], [BK[4 + p_], "spTa"])
                act(spTa[0:16, p_, :, 64:80], src3[0:16, :, 64:80], AF.Exp, 0.0, 1.0, [], [BK[4 + p_], "spTa"])
                act(spTm[:, p_, :], banks[6 + p_][:, 0:64], AF.Exp, 0.0, 0.125, [], [BK[6 + p_], "spTm"])
            for m in range(2):
                for hd in range(4):
                    bn = 2 * m + hd // 2
                    ob = banks[bn][0:16, (hd % 2) * 256:(hd % 2) * 256 + 129]
                    for kt in range(8):
                        mm(ob, spTb[:, bn, hd % 2, kt * 16:(kt + 1) * 16], cvb[:, kt, hd, 0:129], kt == 0 and hd % 2 == 0, False,
                           ["spTb", "cvb"], [BK[bn]], skip=True)
                    mm(ob, spTb[0:16, bn, hd % 2, 128:144], sbva[0:16, hd, 0:129], False, True, ["spTb", "sbva"], [BK[bn]], skip=True)
            for hd in range(4):
                oa = banks[4][0:16, hd * 80:hd * 80 + 65]
                for kt in range(4):
                    mm(oa, spTa[:, hd % 2, hd // 2, kt * 16:(kt + 1) * 16], cva[:, kt, hd, 0:65], kt == 0 and hd == 0, False,
                       ["spTa", "cva"], [BK[4]], skip=True)
                mm(oa, spTa[0:16, hd % 2, hd // 2, 64:80], sava[0:16, hd, 0:65], False, True, ["spTa", "sava"], [BK[4]], skip=True)
                om = banks[6][0:16, hd * 80:hd * 80 + 65]
                for kt in range(2):
                    mm(om, spTm[:, hd % 2, (hd // 2) * 32 + kt * 16:(hd // 2) * 32 + (kt + 1) * 16], cvm[:, kt, hd, 0:65],
                       kt == 0 and hd == 0, kt == 1, ["spTm", "cvm"], [BK[6]], skip=True)
            norm_diff4(pairs[0][0:16, :].rearrange("p (q c) -> p q c", q=4), pairs[1][0:16, :].rearrange("p (q c) -> p q c", q=4),
                       [BK[0], BK[1]], [BK[2], BK[3]], mixs[0:16, 256:768].rearrange("p (q c) -> p q c", q=4), 16)
            for bn, c_lo in ((4, 0), (6, 768)):
                o3 = banks[bn][0:16, 0:320].rearrange("p (h c) -> p h c", h=4)
                recip(nsc[0:16, 20:24], o3[:, :, 64], [], [BK[bn], "sn_r"])
                tt("dve", mixs[0:16, c_lo:c_lo + 256].rearrange("p (h c) -> p h c", h=4), o3[:, :, 0:64],
                   nsc[0:16, 20:24].unsqueeze(2).to_broadcast([16, 4, 64]), ALU.mult, ["sn_r"], [BK[bn], "mixed"])
            out_phase(mixs[0:16, :], sgs[0:16, :], xs[tok, :], ys[tok, :], 16, 0, 1, 2)
        P.emit(block)
    return nc


def _rope_tables(pos):
    inv = (10000.0 ** (-np.arange(32, dtype=np.float64) / 32.0)).astype(np.float32)
    ang = (np.asarray(pos).astype(np.float32)[..., None] * inv).astype(np.float32).astype(np.float64)
    return np.concatenate([np.cos(ang), np.sin(ang)], -1).astype(np.float32)


def _core_inputs(inp, core, nspan):
    b, h = core // 2, core % 2
    S = 1024 * nspan
    NT = S // 128
    xp = inp["x_prompt"][b]
    T = np.arange(NT)
    own = 2 * T + h
    oth = 2 * T + 1 - h
    idx_all = (np.stack([own, oth], 1)[:, :, None] * 64 + np.arange(64)).reshape(-1)
    idx_own = (own[:, None] * 64 + np.arange(64)).reshape(-1)
    d = {}
    d["xT_all"] = np.ascontiguousarray(xp[idx_all].T)
    d["xT_own"] = np.ascontiguousarray(xp[idx_own].T)
    d["x_own"] = np.ascontiguousarray(xp[idx_own])
    d["memT"] = np.ascontiguousarray(inp["mem_prompt"][b].T)
    d["w_in"] = np.ascontiguousarray(inp["w_in"][0]); d["w_mem"] = np.ascontiguousarray(inp["w_mem_kv"][0])
    d["w_out"] = np.ascontiguousarray(inp["w_out"][0])
    d["cs_kv"] = np.ascontiguousarray(_rope_tables(idx_all.reshape(NT, 128)).transpose(1, 0, 2).reshape(128, NT * 64))
    NU = NT // 2
    d["cs_q"] = np.ascontiguousarray(_rope_tables(idx_own.reshape(NU, 128)).transpose(1, 0, 2).reshape(128, NU * 64))
    past = inp["cache_b_k"].shape[2]
    d["cs_s"] = _rope_tables(past + np.arange(16))
    table = inp["a_rel_bias"][0]
    kk = np.arange(128)
    perm = np.where(kk < 64, 64 * h + kk, 64 * (1 - h) + kk - 64)
    q = np.arange(64)
    ab = np.empty((128, 4, 5, 64), np.float32)
    for jb in range(5):
        dist = 512 - 128 * jb + 64 * h + q[None, :] - perm[:, None]
        kchunk_rel = 2 * (jb - 4) + perm // 64 - h
        valid = (kchunk_rel >= -8) & (kchunk_rel <= 0)
        g = table[:, np.clip(dist, -128, 128) + 128]
        g = np.where(valid[None, :, None], g, np.float32(NEG))
        ab[:, :, jb, :] = g.transpose(1, 0, 2)
    d["abias"] = ab.reshape(128, -1)
    sab = np.zeros((128, 4, 5, 16), np.float32)
    t = np.arange(16)
    for jb in range(5):
        ki = 128 * jb + kk if jb < 4 else 512 + np.minimum(kk, 15)
        dist = t[None, :] + 512 - ki[:, None]
        g = table[:, np.clip(dist, -128, 128) + 128]
        sab[:, :, jb, :] = g.transpose(1, 0, 2)
    d["sabias"] = sab.reshape(128, -1)
    db = np.zeros((128, 1), np.float32)
    if h == 0:
        db[64:] = NEG
    d["dbias"] = db
    d["dlam"] = np.ascontiguousarray(inp["diff_lambda"][0].reshape(1, 256))
    d["gsub"] = np.ascontiguousarray(inp["diff_subln_g"][0].reshape(1, 128))
    d["lng"] = np.ascontiguousarray(inp["ln_g"][0].reshape(1, 1024)); d["lnb"] = np.ascontiguousarray(inp["ln_b"][0].reshape(1, 1024))
    sb = slice(4 * core, 4 * core + 4)
    xs = inp["x_sample"][sb].reshape(64, 1024)
    d["xsT"] = np.ascontiguousarray(xs.T); d["xs"] = np.ascontiguousarray(xs)
    d["cbkT"] = np.ascontiguousarray(inp["cache_b_k"][0, sb].reshape(4, -1, 4, 128).transpose(0, 2, 3, 1))
    d["cbv"] = np.ascontiguousarray(inp["cache_b_v"][0, sb].reshape(4, -1, 128, 512).transpose(0, 2, 1, 3).reshape(4, 128, -1))
    d["cakT"] = np.ascontiguousarray(inp["cache_a_k"][0, sb].reshape(4, -1, 2, 128).transpose(0, 2, 3, 1))
    d["cav"] = np.ascontiguousarray(inp["cache_a_v"][0, sb].reshape(4, -1, 128, 256).transpose(0, 2, 1, 3).reshape(4, 128, -1))
    d["cmkT"] = np.ascontiguousarray(inp["cache_mem_k"][0, sb].reshape(4, -1, 2, 128).transpose(0, 2, 3, 1))
    d["cmv"] = np.ascontiguousarray(inp["cache_mem_v"][0, sb].reshape(4, -1, 128, 256).transpose(0, 2, 1, 3).reshape(4, 128, -1))
    return {k: np.ascontiguousarray(v, dtype=np.float32) for k, v in d.items()}


def _assemble(res, B, S, DB):
    NT = S // 128
    y = np.empty((B, S, 1024), np.float32); bk = np.empty((B, S, 512), np.float32); bv = np.empty((B, S, 512), np.float32)
    ak = np.empty((B, 512, 256), np.float32); av = np.empty((B, 512, 256), np.float32)
    mk = np.empty((B, 256, 256), np.float32); mv = np.empty((B, 256, 256), np.float32)
    ysm = np.empty((DB, 16, 1024), np.float32)
    sak = np.empty((DB, 16, 256), np.float32); sav = np.empty((DB, 16, 256), np.float32)
    sbk = np.empty((DB, 16, 512), np.float32); sbv = np.empty((DB, 16, 512), np.float32)
    for core, r in enumerate(res):
        b, h = core // 2, core % 2
        own = 2 * np.arange(NT) + h
        idx_own = (own[:, None] * 64 + np.arange(64)).reshape(-1)
        y[b, idx_own] = r["y_own"]; bk[b, idx_own] = r["bk_own"]; bv[b, idx_own] = r["bv_own"]
        last = idx_own[-256:] - (S - 512)
        ak[b, last] = r["ak_own"]; av[b, last] = r["av_own"]
        if h == 0:
            mk[b] = r["mk_o"]; mv[b] = r["mv_o"]
        sb = slice(4 * core, 4 * core + 4)
        ysm[sb] = r["ys"].reshape(4, 16, 1024)
        sak[sb] = r["sak"].reshape(4, 16, 256); sav[sb] = r["sav"].reshape(4, 16, 256)
        sbk[sb] = r["sbk"].reshape(4, 16, 512); sbv[sb] = r["sbv"].reshape(4, 16, 512)
    return (y, ysm, ak.reshape(1, B, 512, 4, 64), av.reshape(1, B, 512, 4, 64), bk.reshape(1, B, S, 4, 2, 64),
            bv.reshape(1, B, S, 4, 128), mk.reshape(1, B, 256, 4, 64), mv.reshape(1, B, 256, 4, 64),
            sak.reshape(1, DB, 16, 4, 64), sav.reshape(1, DB, 16, 4, 64), sbk.reshape(1, DB, 16, 4, 2, 64),
            sbv.reshape(1, DB, 16, 4, 128))


def kernel(**inputs):
    inp = {k: np.asarray(v) for k, v in inputs.items()}
    B, S, _ = inp["x_prompt"].shape
    DB = inp["x_sample"].shape[0]
    nspan = S // 1024
    nc = build(nspan)
    in_maps = [_core_inputs(inp, c, nspan) for c in range(N_CORES)]
    res = run_bass_kernel_spmd(nc, in_maps, core_ids=list(range(N_CORES)))
    return _assemble(res.results, B, S, DB)
```
